# Optimizing a Trainium2 kernel written in Bass

```python
import math
import jax, jax.numpy as jnp
from jax import lax
import numpy as np

D_MODEL = 1024
BATCH = 8
SEQ = 4096
DEPTH = 1

ATT_PATTERNS = ((128, 1), (512, 4), (2048, 16))
ATT_GROUPS = len(ATT_PATTERNS)
ATT_HEADS = 8
ATT_HEAD_DIM = 64
ATT_WIDTH = ATT_HEADS * ATT_HEAD_DIM
ATT_BLOCK = 128
RWKV_WIDTH = D_MODEL
RWKV_HEAD_DIM = 64
RWKV_HEADS = RWKV_WIDTH // RWKV_HEAD_DIM
DECAY_LORA = 64
AAA_LORA = 64
GATE_LORA = 160
D_FF = 2816
CONV_WIDTH = 3
N_BRANCHES = 2
ATT_IN = ATT_GROUPS * 3 * ATT_WIDTH
RWKV_IN = 3 * RWKV_WIDTH + DECAY_LORA + AAA_LORA + GATE_LORA
GATE_IN = N_BRANCHES * D_MODEL
N_IN = ATT_IN + RWKV_IN + GATE_IN
RMS_EPS = 1e-6
GN_EPS = 64e-5

kernel_name = 'hybrid_dilated_attn_rwkv7_convffn_adaln'


def rms_norm(x, w):
    xf = x.astype(jnp.float32)
    y = xf * lax.rsqrt(jnp.mean(xf * xf, axis=-1, keepdims=True) + RMS_EPS)
    return (y * w).astype(x.dtype)


def dilated_window_attention(q, k, v, window, dilation):
    b, s, h, e = q.shape
    back = window // dilation
    sub_len = -(-s // dilation)
    n_blk = -(-sub_len // ATT_BLOCK)
    s_pad = n_blk * ATT_BLOCK * dilation

    def to_blocks(t):
        t = jnp.pad(t, ((0, 0), (0, s_pad - s), (0, 0), (0, 0)))
        return t.reshape(b, n_blk, ATT_BLOCK, dilation, h, e)

    def with_prev(t):
        prev = jnp.concatenate([jnp.zeros_like(t[:, :1]), t[:, :-1]], axis=1)
        return jnp.concatenate([prev, t], axis=2)

    qb = to_blocks(q)
    kc = with_prev(to_blocks(k))
    vc = with_prev(to_blocks(v))
    scores = jnp.einsum('bnqrhe,bnkrhe->bnrhqk', qb, kc).astype(jnp.float32) * (e ** -0.5)
    qi = jnp.arange(ATT_BLOCK)[:, None]
    kj = jnp.arange(2 * ATT_BLOCK)[None, :]
    dist = qi + ATT_BLOCK - kj
    kpos = jnp.arange(n_blk)[:, None] * ATT_BLOCK + kj - ATT_BLOCK
    valid = ((dist >= 0) & (dist <= back))[None] & (kpos >= 0)[:, None, :]
    scores = jnp.where(valid[None, :, None, None], scores, -jnp.inf)
    m = jnp.max(scores, axis=-1, keepdims=True)
    p = jnp.exp(scores - m)
    den = jnp.sum(p, axis=-1)
    num = jnp.einsum('bnrhqk,bnkrhe->bnqrhe', p, vc.astype(jnp.float32))
    den_q = jnp.transpose(den, (0, 1, 4, 2, 3))
    out = (num / den_q[..., None]).reshape(b, s_pad, h, e)[:, :s]
    lse = (jnp.transpose(m[..., 0], (0, 1, 4, 2, 3)) + jnp.log(den_q)).reshape(b, s_pad, h)[:, :s]
    return out, lse


def attention_mixer(z):
    b, s, _ = z.shape
    qkv = z.reshape(b, s, ATT_GROUPS, 3, ATT_HEADS, ATT_HEAD_DIM)
    outs, lses = [], []
    for g, (window, dilation) in enumerate(ATT_PATTERNS):
        o, l = dilated_window_attention(qkv[:, :, g, 0], qkv[:, :, g, 1], qkv[:, :, g, 2], window, dilation)
        outs.append(o)
        lses.append(l)
    wts = jax.nn.softmax(jnp.stack(lses), axis=0)
    out = jnp.sum(wts[..., None] * jnp.stack(outs), axis=0)
    return out.reshape(b, s, ATT_WIDTH).astype(z.dtype)


def rwkv7_mixer(z, mu, w0, w2, a0, a2, g2, k_k, k_a, r_k, lnx_w, lnx_b):
    f32 = jnp.float32
    b, s, _ = z.shape
    z_prev = jnp.pad(z, ((0, 0), (1, 0), (0, 0)))[:, :s]
    z = z + (z_prev - z) * mu
    c = RWKV_WIDTH
    r, k, v, w_low, a_low, g_low = jnp.split(
        z, [c, 2 * c, 3 * c, 3 * c + DECAY_LORA, 3 * c + DECAY_LORA + AAA_LORA], axis=-1)
    w_log = -jax.nn.softplus(-(w0 + jnp.tanh(w_low) @ w2).astype(f32)) - 0.5
    decay = jnp.exp(-jnp.exp(w_log))
    a = jax.nn.sigmoid((a0 + a_low @ a2).astype(f32))
    g = jax.nn.sigmoid(g_low) @ g2
    k_mod = k.astype(f32) * (1.0 + (a - 1.0) * k_a)

    def heads(t):
        return t.astype(f32).reshape(b, s, RWKV_HEADS, RWKV_HEAD_DIM)

    kk = heads(k * k_k)
    kk = kk / jnp.maximum(jnp.sqrt(jnp.sum(kk * kk, axis=-1, keepdims=True)), 1e-12)
    r_h, k_h, v_h, w_h, a_h = heads(r), heads(k_mod), heads(v), heads(decay), heads(a)

    def step(state, inp):
        r_t, w_t, k_t, v_t, aa_t, bb_t = inp
        sa = jnp.einsum('bhvk,bhk->bhv', state, aa_t)
        state = state * w_t[:, :, None, :] + sa[..., None] * bb_t[:, :, None, :] + v_t[..., None] * k_t[:, :, None, :]
        return state, jnp.einsum('bhvk,bhk->bhv', state, r_t)

    tm = lambda t: jnp.swapaxes(t, 0, 1)
    state0 = jnp.zeros((b, RWKV_HEADS, RWKV_HEAD_DIM, RWKV_HEAD_DIM), f32)
    _, y = lax.scan(step, state0, (tm(r_h), tm(w_h), tm(k_h), tm(v_h), tm(-kk), tm(kk * a_h)))
    y = tm(y)
    mean = jnp.mean(y, axis=-1, keepdims=True)
    var = jnp.mean(jnp.square(y - mean), axis=-1, keepdims=True)
    y = ((y - mean) * lax.rsqrt(var + GN_EPS)).reshape(b, s, c) * lnx_w + lnx_b
    bonus = (jnp.sum(r_h * k_h * r_k, axis=-1, keepdims=True) * v_h).reshape(b, s, c)
    return ((y + bonus) * g).astype(z.dtype)


def conv_ffn(h, w_up, conv_w, conv_b, w_down):
    s = h.shape[1]
    u = h @ w_up
    up = jnp.pad(u, ((0, 0), (CONV_WIDTH - 1, 0), (0, 0)))
    u = conv_b + sum(conv_w[j] * up[:, j:j + s] for j in range(CONV_WIDTH))
    gate, val = jnp.split(u, 2, axis=-1)
    return (jax.nn.silu(gate) * val) @ w_down


def setup_inputs(seed: int = 0) -> dict:
    key = jax.random.key(seed)
    ks = iter(jax.random.split(key, 32))
    f32 = jnp.float32
    L, D, C = DEPTH, D_MODEL, RWKV_WIDTH

    def nrm(shape, scale):
        return jax.random.normal(next(ks), shape, f32) * scale

    ramp = (jnp.arange(C, dtype=f32) / (C - 1)) ** 0.85
    inputs = {}
    inputs['x'] = nrm((BATCH, SEQ, D), 1.0)
    inputs['c'] = nrm((BATCH, D), 1.0)
    inputs['w_ada'] = nrm((L, D, 6 * D), 0.3 * D ** -0.5)
    inputs['b_ada'] = nrm((L, 6 * D), 0.02)
    inputs['norm1_w'] = 1.0 + nrm((L, D), 0.05)
    inputs['w_in'] = nrm((L, D, N_IN), D ** -0.5)
    inputs['b_gate'] = nrm((L, GATE_IN), 0.1)
    inputs['mu_shift'] = jax.random.uniform(next(ks), (L, RWKV_IN), f32)
    inputs['w0'] = -6.5 + 5.0 * ramp + nrm((L, C), 0.1)
    inputs['w2'] = nrm((L, DECAY_LORA, C), 0.1 * DECAY_LORA ** -0.5)
    inputs['a0'] = nrm((L, C), 0.1)
    inputs['a2'] = nrm((L, AAA_LORA, C), AAA_LORA ** -0.5)
    inputs['g2'] = nrm((L, GATE_LORA, C), GATE_LORA ** -0.5)
    inputs['k_k'] = 0.85 + nrm((L, C), 0.05)
    inputs['k_a'] = 1.0 + nrm((L, C), 0.05)
    inputs['r_k'] = nrm((L, RWKV_HEADS, RWKV_HEAD_DIM), 0.1)
    inputs['lnx_w'] = 1.0 + nrm((L, C), 0.05)
    inputs['lnx_b'] = nrm((L, C), 0.02)
    inputs['w_att_out'] = nrm((L, ATT_WIDTH, D), ATT_WIDTH ** -0.5)
    inputs['w_rwkv_out'] = nrm((L, C, D), C ** -0.5)
    inputs['w_o'] = nrm((L, D, D), D ** -0.5)
    inputs['norm2_w'] = 1.0 + nrm((L, D), 0.05)
    inputs['w_up'] = nrm((L, D, 2 * D_FF), D ** -0.5)
    inputs['conv_w'] = nrm((L, CONV_WIDTH, 2 * D_FF), CONV_WIDTH ** -0.5)
    inputs['conv_b'] = nrm((L, 2 * D_FF), 0.02)
    inputs['w_down'] = nrm((L, D_FF, D), D_FF ** -0.5)
    inputs['norm_f_w'] = 1.0 + nrm((D,), 0.05)
    return inputs


def reference(x, c, w_ada, b_ada, norm1_w, w_in, b_gate, mu_shift, w0, w2, a0, a2, g2, k_k, k_a, r_k,
              lnx_w, lnx_b, w_att_out, w_rwkv_out, w_o, norm2_w, w_up, conv_w, conv_b, w_down, norm_f_w):
    for l in range(DEPTH):
        ada = (c @ w_ada[l] + b_ada[l])[:, None, :]
        sh1, sc1, gt1, sh2, sc2, gt2 = jnp.split(ada, 6, axis=-1)
        h = rms_norm(x, norm1_w[l]) * (1.0 + sc1) + sh1
        proj = h @ w_in[l]
        att_in, rwkv_in, gate_in = jnp.split(proj, [ATT_IN, ATT_IN + RWKV_IN], axis=-1)
        y_att = attention_mixer(att_in) @ w_att_out[l]
        y_rwkv = rwkv7_mixer(rwkv_in, mu_shift[l], w0[l], w2[l], a0[l], a2[l], g2[l], k_k[l], k_a[l],
                             r_k[l], lnx_w[l], lnx_b[l]) @ w_rwkv_out[l]
        g_att, g_rwkv = jnp.split(jax.nn.sigmoid(gate_in + b_gate[l]), N_BRANCHES, axis=-1)
        x = x + gt1 * ((g_att * y_att + g_rwkv * y_rwkv) @ w_o[l])
        h = rms_norm(x, norm2_w[l]) * (1.0 + sc2) + sh2
        x = x + gt2 * conv_ffn(h, w_up[l], conv_w[l], conv_b[l], w_down[l])
    return rms_norm(x, norm_f_w)
```

```python
import bisect
import os
from contextlib import ExitStack
import numpy as np
import ml_dtypes
import concourse.bass as bass
import concourse.mybir as mybir
from concourse.bass_utils import run_bass_kernel_spmd

F32 = mybir.dt.float32
BF16 = mybir.dt.bfloat16
ALU = mybir.AluOpType
AF = mybir.ActivationFunctionType
AX = mybir.AxisListType

ENGS = ("pe", "act", "dve", "pool", "sp")
NDMA_SEM = 8


class Region:
    __slots__ = ("name", "w", "r")

    def __init__(self, name=""):
        self.name = name
        self.w = None
        self.r = []


class Op:
    __slots__ = ("eng", "idx", "fn", "inc", "token", "is_dma", "tag")

    def __init__(self, eng, idx, fn, is_dma=False):
        self.eng = eng
        self.idx = idx
        self.fn = fn
        self.inc = None
        self.token = None
        self.is_dma = is_dma
        self.tag = None
        if os.environ.get("KD_TAG"):
            import sys
            f = sys._getframe(2)
            while f is not None and f.f_code.co_name in ("op", "dma", "MM", "TR", "ACT", "TT", "TS", "STT", "CPY", "MSET", "DMA", "lerp", "diag"):
                f = f.f_back
            self.tag = f.f_lineno if f is not None else -1


class Prog:
    def __init__(self, nc):
        self.nc = nc
        self.stream = {e: [] for e in ENGS}
        self.nops = {e: 0 for e in ENGS}
        self.cnt = {e: 0 for e in ENGS}
        self.fin = {e: ([], []) for e in ENGS}
        self.last = {e: None for e in ENGS}
        self.known = {e: {} for e in ENGS}
        self.dma_n = {e: 0 for e in ENGS}
        self.sems = {}
        self._ctx = []
        self.defer = False
        self.pend = []

    def schedule(self):
        import heapq
        pend, self.pend = self.pend, []
        self.defer = False
        n = len(pend)
        preds = [None] * n
        lastw, readers = {}, {}
        for i, (kind, eng, fn, reads, writes, cost, kw) in enumerate(pend):
            p = set()
            for R in reads:
                w = lastw.get(id(R))
                if w is not None:
                    p.add(w)
            for W in writes:
                w = lastw.get(id(W))
                if w is not None:
                    p.add(w)
                p.update(readers.get(id(W), ()))
            p.discard(i)
            preds[i] = p
            for R in reads:
                readers.setdefault(id(R), []).append(i)
            for W in writes:
                lastw[id(W)] = i
                readers[id(W)] = []
        succs = [[] for _ in range(n)]
        npred = [len(p) for p in preds]
        for i, p in enumerate(preds):
            for j in p:
                succs[j].append(i)
        DMA_LAT = 2.5
        ready_t = [0.0] * n
        heaps = {e: [] for e in ENGS}
        for i in range(n):
            if npred[i] == 0:
                heapq.heappush(heaps[pend[i][1]], (0.0, i))
        free = {e: 0.0 for e in ENGS}
        order = []
        done = 0
        while done < n:
            best, be = None, None
            for e in ENGS:
                h = heaps[e]
                if h:
                    st = max(free[e], h[0][0])
                    if best is None or (st, h[0][1]) < best:
                        best, be = (st, h[0][1]), e
            assert be is not None, "scheduler deadlock"
            st = best[0]
            h = heaps[be]
            cand = []
            while h and h[0][0] <= st:
                cand.append(heapq.heappop(h))
            cand.sort(key=lambda x: x[1])
            i = cand[0][1]
            for c in cand[1:]:
                heapq.heappush(h, c)
            kind, eng, fn, reads, writes, cost, kw = pend[i]
            fin = st + cost
            free[be] = fin
            if kind == "dma":
                fin = st + DMA_LAT
            order.append(i)
            done += 1
            for j in succs[i]:
                ready_t[j] = max(ready_t[j], fin)
                npred[j] -= 1
                if npred[j] == 0:
                    heapq.heappush(heaps[pend[j][1]], (ready_t[j], j))
        for i in order:
            kind, eng, fn, reads, writes, cost, kw = pend[i]
            if kind == "op":
                self.op(eng, fn, reads, writes)
            else:
                self.dma(eng, fn[0], fn[1], reads, writes, **kw)

    def alloc_sems(self, stack):
        for e in ENGS:
            self.sems[("c", e)] = stack.enter_context(self.nc.semaphore("c_" + e))
        for e in ("sp", "pool", "act"):
            for k in range(NDMA_SEM):
                self.sems[("d", e, k)] = stack.enter_context(self.nc.semaphore("d_%s%d" % (e, k)))

    def _token(self, op):
        if op.token is not None:
            return op.token
        e = op.eng
        idxs, vals = self.fin[e]
        p = bisect.bisect_left(idxs, op.idx)
        if p < len(idxs):
            return (("c", e), vals[p])
        L = self.last[e]
        assert L is not None and L.idx >= op.idx and not L.is_dma
        self.cnt[e] += 1
        L.inc = (("c", e), 1)
        L.token = (("c", e), self.cnt[e])
        idxs.append(L.idx)
        vals.append(self.cnt[e])
        return L.token

    def _wait(self, eng, tok):
        key, val = tok
        if self.known[eng].get(key, 0) >= val:
            return
        self.known[eng][key] = val
        self.stream[eng].append(("wait", key, val))

    def _deps(self, eng, reads, writes):
        deps = []
        for R in reads:
            if R.w is not None:
                deps.append(R.w)
        for W in writes:
            if W.w is not None:
                deps.append(W.w)
            deps.extend(W.r)
        for d in deps:
            if eng == "pe" and d.eng == "pe" and not d.is_dma:
                continue
            self._wait(eng, self._token(d))

    def _record(self, op, reads, writes):
        for R in reads:
            if not op.is_dma:
                R.r = [x for x in R.r if x.is_dma or x.eng != op.eng]
            R.r.append(op)
        for W in writes:
            W.w = op
            W.r = []

    def op(self, eng, fn, reads=(), writes=(), cost=0.5):
        if getattr(self, "budget", None) is not None:
            if self.budget <= 0:
                return None
            self.budget -= 1
        if self.defer:
            self.pend.append(("op", eng, fn, tuple(reads), tuple(writes), cost, None))
            return None
        self._deps(eng, reads, writes)
        o = Op(eng, self.nops[eng], fn)
        self.nops[eng] += 1
        self.stream[eng].append(o)
        self.last[eng] = o
        self._record(o, reads, writes)
        return o

    def dma(self, q, out, in_, reads=(), writes=(), **kw):
        if self.defer:
            self.pend.append(("dma", q, (out, in_), tuple(reads), tuple(writes), 0.06, kw))
            return None
        self._deps(q, reads, writes)
        n = self.dma_n[q]
        self.dma_n[q] += 1
        k, rnd = n % NDMA_SEM, n // NDMA_SEM
        key = ("d", q, k)
        if rnd > 0:
            self._wait(q, (key, 16 * rnd))
        o = Op(q, self.nops[q], lambda e: e.dma_start(out=out, in_=in_, **kw), is_dma=True)
        self.nops[q] += 1
        o.inc = (key, 16)
        o.token = (key, 16 * (rnd + 1))
        self.stream[q].append(o)
        self._record(o, reads, writes)
        return o

    def finish(self, regions):
        for R in regions:
            if R.w is not None:
                self._wait("sp", self._token(R.w))
            for x in R.r:
                self._wait("sp", self._token(x))

    def emit(self, block):
        nc = self.nc
        handles = {"pe": block.tensor, "act": block.scalar, "dve": block.vector,
                   "pool": block.gpsimd, "sp": block.sync}

        def make(e):
            def body(eng):
                for it in self.stream[e]:
                    if isinstance(it, tuple):
                        eng.wait_ge(self.sems[it[1]], it[2])
                    else:
                        ins = it.fn(eng)
                        if it.tag is not None:
                            try:
                                print("TAG", ins.ins.name, it.tag, flush=True)
                            except Exception as ex:
                                print("TAGERR", ex)
                        if it.inc is not None:
                            ins.then_inc(self.sems[it.inc[0]], it.inc[1])
            return body
        for e in ENGS:
            if self.stream[e]:
                handles[e](make(e))

    def barrier(self):
        toks = []
        for e in ENGS:
            L = self.last[e]
            if L is not None:
                toks.append(self._token(L))
        for q in ("sp", "pool", "act"):
            n = self.dma_n[q]
            for k in range(NDMA_SEM):
                cnt = (n - k + NDMA_SEM - 1) // NDMA_SEM
                if cnt > 0:
                    toks.append((("d", q, k), 16 * cnt))
        for e in ENGS:
            for t in toks:
                self._wait(e, t)

    def flush(self):
        with self.nc.Block() as block:
            self.emit(block)
        self.stream = {e: [] for e in ENGS}


S = 4096
D = 1024
NIN = 10016
DFF = 2816
RMS_EPS = 1e-6
GN_EPS = 64e-5
ATT_D = (1, 4, 16)
RW0 = 4608
GATE0 = 7968
CP = {}
_o = 0
for _n, _w in (("bgate", 16), ("mu_rkv", 24), ("mu_wa", 1), ("mu_g", 2), ("w0", 8), ("a0", 8), ("kk", 8),
               ("ka", 8), ("rk", 8), ("lnw", 8), ("lnb", 8), ("cw0", 44), ("cw1", 44), ("cw2", 44), ("cb", 44)):
    CP[_n] = _o
    _o += _w
NCOL = _o


class T:
    def __init__(self, t, name):
        self.t = t
        self.r = Region(name)

    def __getitem__(self, k):
        return self.t[k]


class KB:
    def __init__(self, nc, debug=False):
        self.nc = nc
        self.P = Prog(nc)
        self.debug = debug
        self.bank_i = 0
        self.ev_i = 0
        self.uid = 0

    def sb(self, st, shape, dt, name=None):
        self.uid += 1
        name = "%s_%d" % (name or "t", self.uid)
        if not hasattr(self, "names"):
            self.names = {}
        self.names.setdefault((name.rsplit("_", 1)[0]), []).append(name)
        return T(st.enter_context(self.nc.sbuf_tensor(name, list(shape), dt)), name)

    def bank(self):
        b = self.bank_i
        self.bank_i = (b + 1) % 8
        return b

    def ev(self):
        self.ev_i ^= 1
        return "act" if self.ev_i else "dve"

    @staticmethod
    def fsz(ap):
        n = 1
        for d_ in ap.shape[1:]:
            n *= d_
        return n

    def MM(self, out, lhsT, rhs, start, stop, rd, wr):
        c = 0.035 + self.fsz(out) / 2400.0 * (4 if lhsT.dtype == F32 else 1)
        self.P.op("pe", lambda e: e.matmul(out, lhsT=lhsT, rhs=rhs, start=start, stop=stop), rd, wr, cost=c)

    def TR(self, out, in_, ident, rd, wr):
        self.P.op("pe", lambda e: e.transpose(out, in_, ident), rd, wr, cost=0.035 + self.fsz(out) / 2400.0 * (4 if in_.dtype == F32 else 1))

    def ACT(self, out, in_, func, rd, wr, bias=None, scale=None, accum=None):
        kw = {}
        if bias is not None:
            kw["bias"] = bias
        if scale is not None:
            kw["scale"] = scale
        if accum is not None:
            kw["accum_out"] = accum
        self.P.op("act", lambda e: e.activation(out=out, in_=in_, func=func, **kw), rd, wr, cost=0.2 + self.fsz(out) / 1200.0)

    def TT(self, eng, out, in0, in1, op, rd, wr):
        self.P.op(eng, lambda e: e.tensor_tensor(out=out, in0=in0, in1=in1, op=op), rd, wr, cost=self.vcost(eng, out, 2))

    def TS(self, eng, out, in0, s1, s2, op0, op1, rd, wr):
        if s2 is None:
            self.P.op(eng, lambda e: e.tensor_scalar(out=out, in0=in0, scalar1=s1, scalar2=None, op0=op0), rd, wr, cost=self.vcost(eng, out, 1))
        else:
            self.P.op(eng, lambda e: e.tensor_scalar(out=out, in0=in0, scalar1=s1, scalar2=s2, op0=op0, op1=op1), rd, wr, cost=self.vcost(eng, out, 1))

    def STT(self, eng, out, in0, scalar, in1, op0, op1, rd, wr):
        self.P.op(eng, lambda e: e.scalar_tensor_tensor(out=out, in0=in0, scalar=scalar, in1=in1, op0=op0, op1=op1), rd, wr, cost=self.vcost(eng, out, 2))

    def CPY(self, eng, out, in_, rd, wr):
        if eng == "act":
            self.P.op("act", lambda e: e.activation(out=out, in_=in_, func=AF.Copy), rd, wr, cost=0.2 + self.fsz(out) / 1200.0)
        else:
            self.P.op(eng, lambda e: e.tensor_copy(out=out, in_=in_), rd, wr, cost=self.vcost(eng, out, 1))

    def MSET(self, eng, out, val, wr):
        self.P.op(eng, lambda e: e.memset(out, val), (), wr, cost=self.vcost(eng, out, 1))

    def vcost(self, eng, out, nin):
        f = self.fsz(out)
        if eng == "pool":
            return 0.3 + f * (2.6 if nin == 2 else (3.4 if f >= 1024 else 1.2)) / 1200.0
        return 0.16 + f / 960.0

    def DMA(self, q, out, in_, rd, wr):
        return self.P.dma(q, out, in_, rd, wr)

    def dram(self, name, shape, dt, kind=None):
        if kind is None and self.debug:
            kind = "ExternalOutput"
        if kind is None:
            return self.nc.dram_tensor(name, list(shape), dt).ap()
        return self.nc.dram_tensor(name, list(shape), dt, kind=kind).ap()

    def setup(self, st):
        nc = self.nc
        di = lambda n, s: nc.dram_tensor(n, list(s), F32, kind="ExternalInput").ap()
        self.x = di("x", [S, D])
        self.ccol_d = di("ccol", [128, 8])
        self.w_ada = di("w_ada", [D, 6 * D])
        self.b_ada = di("b_ada", [1, 6 * D])
        self.norm1_w = di("norm1_w", [1, D])
        self.norm2_w = di("norm2_w", [1, D])
        self.norm_f_w = di("norm_f_w", [1, D])
        self.w_in = di("w_in", [D, NIN])
        self.w_att_out = di("w_att_out", [512, D])
        self.w_rwkv_out = di("w_rwkv_out", [D, D])
        self.w_o = di("w_o", [D, D])
        self.w_up = di("w_up", [D, 2 * DFF])
        self.w_down = di("w_down", [DFF, D])
        self.w2a2_d = di("w2a2", [128, D])
        self.g2_d = di("g2", [160, D])
        self.colp_d = di("colp", [128, NCOL])
        self.ident_d = di("ident", [128, 128])
        self.maskA_d = di("maskA", [128, 256])
        self.mask2_d = di("mask2", [128, 192])
        self.bones_d = di("bones", [128, 128])
        self.sel_d = di("sel65", [128, 64])
        self.scanm_d = di("scanm", [128, 256])
        self.out = nc.dram_tensor("out", [S, D], F32, kind="ExternalOutput").ap()
        self.PF = self.dram("PF", [NIN, S], BF16, kind=("ExternalInput" if os.environ.get("KD_ONLY") else None))
        self.VT = self.dram("VT", [3, S, 512], BF16)
        self.YAT = self.dram("YAT", [512, S], BF16)
        self.YRT = self.dram("YRT", [D, S], BF16)
        self.X1 = self.dram("X1", [S, D], F32)
        self.ACTS = self.dram("ACTS", [DFF, S], BF16)
        self.ps = st.enter_context(nc.psum_tensor("ps", [128, 8, 512], F32))
        self.PB = [Region("psb%d" % i) for i in range(8)]
        self.ident_f = self.sb(st, [128, 128], F32, "identf")
        self.ident_b = self.sb(st, [128, 128], BF16, "identb")
        self.colp = self.sb(st, [128, NCOL], F32, "colp")
        self.GT1 = self.sb(st, [128, D], F32, "gt1")
        self.GT2 = self.sb(st, [128, D], F32, "gt2")
        self.A1c = self.sb(st, [128, 8], F32, "a1c")
        self.S1c = self.sb(st, [128, 8], F32, "s1c")
        self.A2c = self.sb(st, [128, 8], F32, "a2c")
        self.S2c = self.sb(st, [128, 8], F32, "s2c")
        self.epsc = self.sb(st, [128, 2], F32, "epsc")
        self.DMA("sp", self.ident_f[:], self.ident_d, (), [self.ident_f.r])
        self.DMA("sp", self.colp[:], self.colp_d, (), [self.colp.r])
        self.CPY("dve", self.ident_b[:], self.ident_f[:], [self.ident_f.r], [self.ident_b.r])
        self.MSET("pool", self.epsc[:, 0:1], RMS_EPS, [self.epsc.r])
        self.MSET("pool", self.epsc[:, 1:2], GN_EPS, [self.epsc.r])

    def col(self, name, j=0, rows=slice(0, 128)):
        return self.colp[rows, CP[name] + j:CP[name] + j + 1]

    def phase0(self):
        ps, PB = self.ps, self.PB
        with ExitStack() as st:
            ccol = self.sb(st, [128, 8], F32, "ccol")
            bada = self.sb(st, [128, 6 * D], F32, "bada")
            ADA = self.sb(st, [128, 6 * D], F32, "ada")
            wa = [self.sb(st, [128, 3072], F32, "wa") for _ in range(3)]
            nw = [self.sb(st, [128, D], F32, "nw") for _ in range(2)]
            tmp = self.sb(st, [128, D], F32, "tmp")
            tmp2 = self.sb(st, [128, D], F32, "tmp2")
            self.DMA("sp", ccol[:], self.ccol_d, (), [ccol.r])
            self.DMA("pool", bada[:], self.b_ada.partition_broadcast(128), (), [bada.r])
            self.DMA("pool", nw[0][:], self.norm1_w.partition_broadcast(128), (), [nw[0].r])
            self.DMA("pool", nw[1][:], self.norm2_w.partition_broadcast(128), (), [nw[1].r])
            i = 0
            for half in range(2):
                for kc in range(8):
                    buf = wa[i % 3]
                    i += 1
                    self.DMA("sp", buf[:], self.w_ada[kc * 128:(kc + 1) * 128, half * 3072:(half + 1) * 3072], (), [buf.r])
                    for j in range(6):
                        self.MM(ps[:, j, :], ccol[:, kc:kc + 1].to_broadcast([128, 128]), buf[:, j * 512:(j + 1) * 512],
                                kc == 0, kc == 7, [ccol.r, buf.r], [PB[j]])
                for j in range(6):
                    o = half * 3072 + j * 512
                    self.TT("dve", ADA[:, o:o + 512], ps[:, j, :], bada[:, o:o + 512], ALU.add, [PB[j], bada.r], [ADA.r])
            idb = self.ident_f[:].unsqueeze(1).to_broadcast([128, 8, 128])
            v3 = lambda t: t[:].rearrange("p (k f) -> p k f", f=128)

            def diag(dst, src_ap_region, src):
                self.TT("dve", v3(tmp2), src, idb, ALU.mult, [src_ap_region, self.ident_f.r], [tmp2.r])
                self.P.op("dve", lambda e: e.tensor_reduce(out=dst[:], in_=v3(tmp2), axis=AX.X, op=ALU.add), [tmp2.r], [dst.r])
            for (sc_o, sh_o, nwt, Ac, Sc) in ((1024, 0, nw[0], self.A1c, self.S1c), (4096, 3072, nw[1], self.A2c, self.S2c)):
                self.STT("dve", tmp[:], ADA[:, sc_o:sc_o + D], 1.0, nwt[:], ALU.add, ALU.mult, [ADA.r, nwt.r], [tmp.r])
                diag(Ac, tmp.r, v3(tmp))
                diag(Sc, ADA.r, ADA[:, sh_o:sh_o + D].rearrange("p (k f) -> p k f", f=128))
            self.CPY("dve", self.GT1[:], ADA[:, 2048:3072], [ADA.r], [self.GT1.r])
            self.CPY("dve", self.GT2[:], ADA[:, 5120:6144], [ADA.r], [self.GT2.r])
            self.P.barrier()
            self.P.flush()

    def norm_block(self, xt, xs, junk, ssq, rs):
        self.MSET("pool", ssq[:], 0.0, [ssq.r])
        self.ACT(junk[:], xt[:], AF.Square, [xt.r, ssq.r], [junk.r, ssq.r], accum=ssq[:])
        self.ACT(rs[:], ssq[:], AF.Sqrt, [ssq.r, self.epsc.r], [rs.r], bias=self.epsc[:, 0:1], scale=1.0 / D)
        self.P.op("dve", lambda e: e.reciprocal(out=rs[:], in_=rs[:]), [rs.r], [rs.r], cost=0.2)
        self.ACT(xs[:], xt[:], AF.Identity, [xt.r, rs.r], [xs.r], scale=rs[:, 0:1])

    def transpose_group(self, xs2, g, Ac, Sc, hT, W=256):
        ps, PB = self.ps, self.PB
        banks = [self.bank() for _ in range(4)]
        for blk in range(2):
            for kc in range(8):
                b = banks[kc // 2]
                o = (kc % 2) * 256 + blk * 128
                self.TR(ps[:, b, o:o + 128], xs2[blk][:, kc * 128:(kc + 1) * 128], self.ident_f[:],
                        [xs2[blk].r, self.ident_f.r], [PB[b]])
        for kc in range(8):
            b = banks[kc // 2]
            o = (kc % 2) * 256
            self.ACT(hT[:, kc, g * 256:(g + 1) * 256], ps[:, b, o:o + 256], AF.Identity, [PB[b], Ac.r, Sc.r], [hT.r],
                     bias=Sc[:, kc:kc + 1], scale=Ac[:, kc:kc + 1])

    def phaseA(self, hT):
        with ExitStack() as st:
            xt = [self.sb(st, [128, D], F32, "xt") for _ in range(3)]
            xs = [self.sb(st, [128, D], F32, "xs") for _ in range(4)]
            junk = self.sb(st, [128, D], F32, "junk")
            ssq = [self.sb(st, [128, 1], F32, "ssq") for _ in range(2)]
            rs = [self.sb(st, [128, 1], F32, "rs") for _ in range(2)]
            if os.environ.get('KD_SCHED', '1') == '1':
                self.P.defer = True
            for g in range(S // 256):
                pair = []
                for blk in range(2):
                    i = g * 2 + blk
                    self.DMA("sp", xt[i % 3][:], self.x[i * 128:(i + 1) * 128, :], (), [xt[i % 3].r])
                    self.norm_block(xt[i % 3], xs[i % 4], junk, ssq[i % 2], rs[i % 2])
                    pair.append(xs[i % 4])
                self.transpose_group(pair, g, self.A1c, self.S1c, hT)
            if self.P.defer:
                self.P.schedule()
            self.P.barrier()
            self.P.flush()

    def phaseB(self, hT):
        ps, PB = self.ps, self.PB
        blocks = []
        for g in range(3):
            blocks.append((g * 1536, 512, "f"))
            blocks.append((g * 1536 + 512, 512, "f"))
            blocks.append((g * 1536 + 1024, 512, ("v", g)))
        for c0 in range(RW0, 7680, 512):
            blocks.append((c0, 512, "f"))
        blocks.append((7680, 288, "f"))
        for c0 in range(GATE0, NIN, 512):
            blocks.append((c0, 512, "f"))
        with ExitStack() as st:
            wf = [self.sb(st, [128, 8, 512], F32, "wf") for _ in range(2)]
            wb = [self.sb(st, [128, 8, 512], BF16, "wb") for _ in range(2)]
            stg = [self.sb(st, [128, 4, 512], BF16, "stg") for _ in range(4)]
            si = 0
            for bi, (c0, cw, kind) in enumerate(blocks):
                f_, b_ = wf[bi % 2], wb[bi % 2]
                self.DMA("pool", f_[:, :, 0:cw], self.w_in[:, c0:c0 + cw].rearrange("(k p) n -> p k n", p=128), (), [f_.r])
                self.CPY("pool", b_[:, :, 0:cw], f_[:, :, 0:cw], [f_.r], [b_.r])
                for tt in range(8):
                    sg = stg[si % 4]
                    si += 1
                    t0 = tt * 512
                    if kind == "f":
                        nch = (cw + 127) // 128
                        for oc in range(nch):
                            w = min(128, cw - oc * 128)
                            b = self.bank()
                            for kc in range(8):
                                self.MM(ps[0:w, b, :], b_[:, kc, oc * 128:oc * 128 + w], hT[:, kc, t0:t0 + 512],
                                        kc == 0, kc == 7, [b_.r, hT.r], [PB[b]])
                            self.CPY(self.ev(), sg[0:w, oc, :], ps[0:w, b, :], [PB[b]], [sg.r])
                        if cw == 512:
                            self.DMA("sp", self.PF[c0:c0 + 512, t0:t0 + 512].rearrange("(o p) t -> p o t", p=128), sg[:], [sg.r], [Region()])
                        else:
                            for oc in range(nch):
                                w = min(128, cw - oc * 128)
                                self.DMA("sp", self.PF[c0 + oc * 128:c0 + oc * 128 + w, t0:t0 + 512], sg[0:w, oc, :], [sg.r], [Region()])
                    else:
                        g = kind[1]
                        for sbk in range(4):
                            b = self.bank()
                            for kc in range(8):
                                self.MM(ps[:, b, :], hT[:, kc, t0 + sbk * 128:t0 + (sbk + 1) * 128], b_[:, kc, :],
                                        kc == 0, kc == 7, [b_.r, hT.r], [PB[b]])
                            self.CPY(self.ev(), sg[:, sbk, :], ps[:, b, :], [PB[b]], [sg.r])
                        self.DMA("sp", self.VT[g, t0:t0 + 512, :].rearrange("(s p) f -> p s f", p=128), sg[:], [sg.r], [Region()])
            self.P.barrier()
            self.P.flush()


def _colv(v):
    v = np.asarray(v, np.float32).reshape(-1)
    n = (v.size + 127) // 128
    buf = np.zeros(n * 128, np.float32)
    buf[:v.size] = v
    return np.ascontiguousarray(buf.reshape(n, 128).T)


def build_nc(stop_after=None, debug=False):
    nc = bass.Bass("TRN2", target_bir_lowering=False)
    kb = KB(nc, debug=debug)
    build_nc.kb = kb
    with ExitStack() as st:
        kb.P.alloc_sems(st)
        kb.setup(st)
        kb.run(st, stop_after)
    return nc


def make_inputs(inp):
    f = lambda a: np.ascontiguousarray(np.asarray(a, np.float32))
    mu = f(inp["mu_shift"][0])
    colp = np.zeros((128, NCOL), np.float32)

    def put(name, arr):
        colp[:, CP[name]:CP[name] + arr.shape[1]] = arr
    put("bgate", _colv(inp["b_gate"][0]))
    put("mu_rkv", _colv(mu[0:3072]))
    put("mu_wa", _colv(mu[3072:3200]))
    put("mu_g", _colv(mu[3200:3360]))
    put("w0", _colv(inp["w0"][0])); put("a0", _colv(inp["a0"][0])); put("kk", _colv(inp["k_k"][0]))
    put("ka", _colv(inp["k_a"][0])); put("rk", _colv(inp["r_k"][0])); put("lnw", _colv(inp["lnx_w"][0]))
    put("lnb", _colv(inp["lnx_b"][0]))
    cw = f(inp["conv_w"][0])
    put("cw0", _colv(cw[0])); put("cw1", _colv(cw[1])); put("cw2", _colv(cw[2])); put("cb", _colv(inp["conv_b"][0]))
    k = np.arange(128)[:, None]
    q = np.arange(128)[None, :]
    NEG = -30000.0
    maskA = np.concatenate([np.where(k <= q, 0.0, NEG), np.where(k >= q, 0.0, NEG)], axis=1).astype(np.float32)
    j = (np.arange(128) % 64)[:, None]
    i = np.arange(64)[None, :]
    mask2 = np.concatenate([(i > j), (i >= j), (i < j)], axis=1).astype(np.float32)
    bones = np.zeros((128, 128), np.float32)
    bones[0:64, 0:64] = 1.0
    bones[64:128, 64:128] = 1.0
    sel = np.zeros((128, 64), np.float32)
    sel[64, :] = 1.0
    scanm = np.ones((128, 256), np.float32)
    scanm[:, 0::64] = 0.0
    shared = {
        "w_ada": f(inp["w_ada"][0]), "b_ada": f(inp["b_ada"][0]).reshape(1, -1),
        "norm1_w": f(inp["norm1_w"][0]).reshape(1, -1), "norm2_w": f(inp["norm2_w"][0]).reshape(1, -1),
        "norm_f_w": f(inp["norm_f_w"]).reshape(1, -1), "w_in": f(inp["w_in"][0]),
        "w_att_out": f(inp["w_att_out"][0]), "w_rwkv_out": f(inp["w_rwkv_out"][0]), "w_o": f(inp["w_o"][0]),
        "w_up": f(inp["w_up"][0]), "w_down": f(inp["w_down"][0]),
        "w2a2": np.ascontiguousarray(np.concatenate([f(inp["w2"][0]), f(inp["a2"][0])], axis=0)),
        "g2": f(inp["g2"][0]), "colp": colp, "ident": np.eye(128, dtype=np.float32), "maskA": maskA,
        "mask2": mask2, "bones": bones, "sel65": sel, "scanm": scanm,
    }
    maps = []
    for b in range(8):
        m = dict(shared)
        m["x"] = f(inp["x"][b])
        m["ccol"] = _colv(inp["c"][b])
        maps.append(m)
    return maps


def kernel(**inputs):
    nc = build_nc()
    maps = make_inputs(inputs)
    res = run_bass_kernel_spmd(nc, maps, core_ids=list(range(8)))
    return np.stack([np.asarray(r["out"], np.float32) for r in res.results], axis=0)


def _run(self, st, stop_after=None):
    if os.environ.get("KD_ONLY") == "D":
        self.P.barrier()
        self.P.flush()
        self.phaseD()
        return
    self.phase0()
    with ExitStack() as s1:
        hT = self.sb(s1, [128, 8, S], BF16, "hT")
        self.phaseA(hT)
        if stop_after == "A":
            return
        self.phaseB(hT)
    if stop_after == "B":
        return
    if stop_after != "D":
        self.phaseC()
    if stop_after == "C":
        return
    self.phaseD()
    if stop_after == "D":
        return
    with ExitStack() as s2:
        hT2 = self.sb(s2, [128, 8, S], BF16, "hT2")
        self.phaseE(hT2)
        if stop_after == "E":
            return
        self.phaseF1(hT2)
    if stop_after == "F1":
        return
    self.phaseF2()


KB.run = _run


def _phaseC(self):
    ps, PB = self.ps, self.PB
    with ExitStack() as st:
        maskf = self.sb(st, [128, 256], F32, "maskf")
        maskb = self.sb(st, [128, 256], BF16, "maskb")
        sel = self.sb(st, [128, 64], F32, "sel")
        qk = [self.sb(st, [128, 2, S], BF16, "qk") for _ in range(2)]
        Vs = [self.sb(st, [128, 32, 2, 65], BF16, "Vs") for _ in range(2)]
        ACC = self.sb(st, [128, 2, S], F32, "ACC")
        PT = self.sb(st, [128, 4, 16, 256], BF16, "PT")
        PTr = [[Region("pt") for _ in range(16)] for _ in range(4)]
        rec = [self.sb(st, [64, 512], F32, "rec") for _ in range(2)]
        ya = [self.sb(st, [64, 512], BF16, "ya") for _ in range(2)]
        self.DMA("sp", maskf[:], self.maskA_d, (), [maskf.r])
        self.DMA("sp", sel[:], self.sel_d, (), [sel.r])
        self.CPY("dve", maskb[:], maskf[:], [maskf.r], [maskb.r])
        for v in Vs:
            self.MSET("pool", v[:], 1.0, [v.r])
        if os.environ.get('KD_SCHED', '1') == '1':
            self.P.defer = True
        sb_i = [0]
        pb_i = [0]

        def sbank():
            b = sb_i[0]
            sb_i[0] = (b + 1) % 6
            return b

        it = 0
        fin_i = 0
        for hp in range(4):
            for g in range(3):
                d = ATT_D[g]
                nb = S // (128 * d)
                buf, V = qk[it % 2], Vs[it % 2]
                it += 1
                c0 = g * 1536 + hp * 128
                self.DMA("sp", buf[:, 0, :], self.PF[c0:c0 + 128, :], (), [buf.r])
                self.DMA("sp", buf[:, 1, :], self.PF[c0 + 512:c0 + 640, :], (), [buf.r])
                vsrc = self.VT[g].rearrange("(n j r) f -> r j n f", j=128, r=d)
                for r in range(d):
                    for h2 in range(2):
                        f0 = hp * 128 + h2 * 64
                        self.DMA("pool", V[:, r * nb:(r + 1) * nb, h2, 0:64], vsrc[r, :, :, f0:f0 + 64], (), [V.r])
                qv = buf[:, 0, :].rearrange("p (n i r) -> p r n i", i=128, r=d)
                kv = buf[:, 1, :].rearrange("p (n i r) -> p r n i", i=128, r=d)
                for h2 in range(2):
                    hs = slice(h2 * 64, (h2 + 1) * 64)
                    state = {"pob": None}

                    def emit_S(kb, r):
                        nq = 2 if kb + 1 < nb else 1
                        b = sbank()
                        self.MM(ps[:, b, 0:nq * 128].rearrange("p (a i) -> p a i", i=128), kv[hs, r, kb, :],
                                qv[hs, r, kb:kb + nq, :], True, False, [buf.r], [PB[b]])
                        self.MM(ps[:, b, 0:nq * 128], self.ident_b[:], maskb[:, 0:nq * 128], False, True,
                                [self.ident_b.r, maskb.r], [PB[b]])
                        self.ACT(PT[:, kb % 4, r, 0:nq * 128], ps[:, b, 0:nq * 128], AF.Exp, [PB[b]], [PTr[kb % 4][r]], scale=0.125)

                    def emit_O(kb, r):
                        slot = (kb % 4) if d == 1 else (r % 4)
                        if slot == 0:
                            state["pob"] = 6 + pb_i[0]
                            pb_i[0] ^= 1
                        pob = state["pob"]
                        po = ps[0:65, pob, slot * 128:(slot + 1) * 128]
                        if kb > 0:
                            self.MM(po, V[:, r * nb + kb - 1, h2, :], PT[:, (kb - 1) % 4, r, 128:256], True, False,
                                    [V.r, PTr[(kb - 1) % 4][r]], [PB[pob]])
                        self.MM(po, V[:, r * nb + kb, h2, :], PT[:, kb % 4, r, 0:128], kb == 0, True,
                                [V.r, PTr[kb % 4][r]], [PB[pob]])
                        if slot == 3:
                            if d == 1:
                                t0 = (kb // 4) * 512
                                av = ACC[0:65, h2, t0:t0 + 512]
                                pv = ps[0:65, pob, :]
                            else:
                                av = ACC[0:65, h2, kb * 128 * d:(kb + 1) * 128 * d].rearrange("p (i r) -> p r i", r=d)[:, r - 3:r + 1, :]
                                pv = ps[0:65, pob, :].rearrange("p (r i) -> p r i", i=128)
                            if g == 0:
                                self.CPY("dve", av, pv, [PB[pob]], [ACC.r])
                            else:
                                self.TT("dve", av, pv, av, ALU.add, [PB[pob], ACC.r], [ACC.r])

                    steps = [(kb, r) for kb in range(nb) for r in range(d)]
                    LAG = 2
                    for i, (kb, r) in enumerate(steps):
                        emit_S(kb, r)
                        if i >= LAG:
                            emit_O(*steps[i - LAG])
                    for j in range(max(0, len(steps) - LAG), len(steps)):
                        emit_O(*steps[j])
            for h2 in range(2):
                for tt in range(8):
                    b = sbank()
                    rc, yy = rec[fin_i % 2], ya[fin_i % 2]
                    fin_i += 1
                    ts_ = slice(tt * 512, (tt + 1) * 512)
                    self.MM(ps[0:64, b, :], sel[0:65, :], ACC[0:65, h2, ts_], True, True, [sel.r, ACC.r], [PB[b]])
                    self.ACT(rc[:], ps[0:64, b, :], AF.Ln, [PB[b]], [rc.r])
                    self.ACT(rc[:], rc[:], AF.Exp, [rc.r], [rc.r], scale=-1.0)
                    self.TT("pool", yy[:], ACC[0:64, h2, ts_], rc[:], ALU.mult, [ACC.r, rc.r], [yy.r])
                    r0 = hp * 128 + h2 * 64
                    self.DMA("sp", self.YAT[r0:r0 + 64, ts_], yy[:], [yy.r], [Region()])
        if self.P.defer:
            self.P.schedule()
        self.P.barrier()
        self.P.flush()


KB.phaseC = _phaseC


C0 = 0.6065306597126334
TW = 256


def _phaseD(self):
    ps, PB = self.ps, self.PB
    P = self.P
    with ExitStack() as st:
        sb = lambda shape, dt, name: self.sb(st, shape, dt, name)
        W2A2 = sb([128, D], BF16, "w2a2")
        G2a = sb([128, D], BF16, "g2a")
        G2b = sb([32, D], BF16, "g2b")
        with ExitStack() as st2:
            wst = self.sb(st2, [128, D], F32, "wst")
            self.DMA("sp", wst[:], self.w2a2_d, (), [wst.r])
            self.CPY("dve", W2A2[:], wst[:], [wst.r], [W2A2.r])
            self.DMA("sp", wst[:], self.g2_d[0:128, :], [], [wst.r])
            self.CPY("dve", G2a[:], wst[:], [wst.r], [G2a.r])
            self.DMA("sp", wst[0:32, :], self.g2_d[128:160, :], [], [wst.r])
            self.CPY("dve", G2b[:], wst[0:32, :], [wst.r], [G2b.r])
            self.P.barrier()
            self.P.flush()
        mask2 = sb([128, 192], F32, "mask2")
        bones = sb([128, 128], F32, "bones")
        scanm = sb([128, TW], F32, "scanm")
        self.DMA("sp", mask2[:], self.mask2_d, (), [mask2.r])
        self.DMA("sp", bones[:], self.bones_d, (), [bones.r])
        self.DMA("sp", scanm[:], self.scanm_d, (), [scanm.r])
        idn64 = sb([128, 64], BF16, "idn64")
        self.TT("dve", idn64[:], self.ident_f[:, 0:64], self.ident_f[:, 64:128], ALU.add, [self.ident_f.r], [idn64.r])
        omk = sb([128, 8], F32, "omk")
        self.TS("dve", omk[:], self.colp[:, CP["ka"]:CP["ka"] + 8], -1.0, 1.0, ALU.mult, ALU.add, [self.colp.r], [omk.r])
        tiny = sb([128, 1], F32, "tiny")
        self.MSET("pool", tiny[:], 1e-24, [tiny.r])
        H32 = sb([128, 8, 64], F32, "H32")
        Hbfs = [sb([128, 8, 64], BF16, "Hbf") for _ in range(2)]
        self.MSET("pool", H32[:], 0.0, [H32.r])
        self.MSET("pool", Hbfs[0][:], 0.0, [Hbfs[0].r])
        ZB = [sb([128, 6, TW + 1], BF16, "ZB") for _ in range(2)]
        ZL = [sb([128, 3, TW + 1], BF16, "ZL") for _ in range(2)]
        tset = [{n: sb([128, 2, TW], F32, n) for n in "R K V D1 D2 D3 SW A kk0 ta".split()} for _ in range(1)]
        tsq = [sb([128, 2, TW], BF16, "tsq") for _ in range(2)]
        trk = [sb([128, 2, TW], BF16, "trk") for _ in range(1)]
        dl = sb([128, TW], F32, "dl")
        bones_b = sb([128, 128], BF16, "bonesb")
        self.CPY("dve", bones_b[:], bones[:], [bones.r], [bones_b.r])
        omu = sb([128, 24], F32, "omu")
        self.TS("dve", omu[:], self.colp[:, CP["mu_rkv"]:CP["mu_rkv"] + 24], -1.0, 1.0, ALU.mult, ALU.add, [self.colp.r], [omu.r])
        WAl = sb([128, TW], F32, "WAl"); G0 = sb([128, TW], F32, "G0"); G1 = sb([32, TW], F32, "G1")
        TWA = sb([128, TW], BF16, "TWA"); SG0 = sb([128, TW], BF16, "SG0"); SG1 = sb([32, TW], BF16, "SG1")
        Gts = [sb([128, 8, TW], BF16, "Gt") for _ in range(2)]
        bonuss = [sb([128, 8, TW], BF16, "bonus") for _ in range(2)]
        PCs = [sb([128, 8, 4], F32, "PC") for _ in range(2)]
        ARs = [sb([128, 8, 4, 2, 64], BF16, "AR") for _ in range(2)]
        Afs = [sb([128, 8, TW], BF16, "Af") for _ in range(2)]; Bfs = [sb([128, 8, TW], BF16, "Bf") for _ in range(2)]
        Kfs = [sb([128, 8, TW], BF16, "Kf") for _ in range(2)]; Vbs = [sb([128, 8, TW], BF16, "Vb") for _ in range(2)]
        Atm = sb([128, 2, D], BF16, "Atm"); Btm = sb([128, 2, D], BF16, "Btm")
        Ktm = sb([128, 2, D], BF16, "Ktm"); Vtm = sb([128, 2, D], BF16, "Vtm")
        BAs = sb([128, 2, 16, 128], BF16, "BAs"); KAs = sb([128, 2, 16, 128], BF16, "KAs")
        NA = [sb([128, 1, 16, 64], BF16, "NA") for _ in range(2)]
        NB = [sb([128, 1, 16, 64], BF16, "NB") for _ in range(2)]
        NT = [sb([128, 1, 16, 64], BF16, "NT") for _ in range(2)]
        M1T = sb([128, 8, 4, 64], BF16, "M1T")
        AKVs = sb([128, 2, 16, 64], BF16, "AKVs")
        U2s = sb([128, 2, 16, 64], BF16, "U2s")
        KVs = sb([128, 8, 4, 64], BF16, "KVs")
        Utm = sb([128, 2, 16, 64], BF16, "Utm")
        class _V:
            def __init__(self, ap, r):
                self.ap, self.r = ap, r

            def __getitem__(self, k):
                return self.ap[k]
        HKf = sb([128, 512], F32, "HK"); T1f = sb([128, 512], F32, "T1")
        HK = _V(HKf[:].rearrange("p (h x) -> p h x", x=64), HKf.r); T1 = _V(T1f[:].rearrange("p (h x) -> p h x", x=64), T1f.r)
        Hy = _V(HKf[:].rearrange("p (q t) -> p q t", q=2), HKf.r); T1y = _V(T1f[:].rearrange("p (q t) -> p q t", q=2), T1f.r)
        Yf = sb([128, 8, TW], F32, "Yf")
        YOb = [sb([128, 2, TW], BF16, "YO") for _ in range(2)]
        ynb = sb([128, 2, TW], F32, "ynb")

        b2_i = [0]

        def bank2():
            b = b2_i[0]
            b2_i[0] = (b + 2) % 8
            return b

        def lerp(dst, zsrc, ci, mucol, rows=slice(0, 128)):
            prev, cur = zsrc[rows, ci, 0:TW], zsrc[rows, ci, 1:TW + 1]
            dd = dl
            self.TT("pool", dd[rows, :], prev, cur, ALU.subtract, [zsrc.r], [dd.r])
            self.STT("dve", dst[rows, :], dd[rows, :], mucol, cur, ALU.mult, ALU.add, [dd.r, zsrc.r, self.colp.r], [dst.r])

        v4 = lambda ap: ap.rearrange("p (n i) -> p n i", i=64)
        ntile = int(os.environ.get('KD_NT', S // TW))
        lim = int(os.environ.get('KD_STAGE', 99))
        def colb(name, c0):
            o = CP[name] + c0
            return self.colp[:, o:o + 2].unsqueeze(2).to_broadcast([128, 2, TW])

        def load_lora(ti):
            Zl = ZL[ti % 2]
            t0 = ti * TW
            for (c_lo, c_hi, rows) in ((0, 2, 128), (2, 3, 32)):
                r0 = RW0 + (24 + c_lo) * 128
                nr = (c_hi - c_lo - 1) * 128 + rows
                if ti == 0:
                    self.MSET("pool", Zl[0:rows, c_lo:c_hi, 0:1], 0.0, [Zl.r])
                    src, dst = self.PF[r0:r0 + nr, 0:TW], Zl[0:rows, c_lo:c_hi, 1:TW + 1]
                else:
                    src, dst = self.PF[r0:r0 + nr, t0 - 1:t0 + TW], Zl[0:rows, c_lo:c_hi, :]
                self.DMA("sp", dst, src.rearrange("(c p) t -> p c t", p=rows), (), [Zl.r])

        def load_batch(ti, bq):
            Zb = ZB[(ti * 4 + bq) % 2]
            t0 = ti * TW
            for j, base in enumerate((0, 8, 16)):
                r0 = RW0 + (base + 2 * bq) * 128
                if ti == 0:
                    self.MSET("pool", Zb[:, 2 * j:2 * j + 2, 0:1], 0.0, [Zb.r])
                    src, dst = self.PF[r0:r0 + 256, 0:TW], Zb[:, 2 * j:2 * j + 2, 1:TW + 1]
                else:
                    src, dst = self.PF[r0:r0 + 256, t0 - 1:t0 + TW], Zb[:, 2 * j:2 * j + 2, :]
                self.DMA("sp", dst, src.rearrange("(c p) t -> p c t", p=128), (), [Zb.r])

        def P_gen(ti):
            t0 = ti * TW
            Zl = ZL[ti % 2]
            AR, Bf, Kf, Af, Vb, Gt, bonus, PC = (X_[ti % 2] for X_ in (ARs, Bfs, Kfs, Afs, Vbs, Gts, bonuss, PCs))
            load_lora(ti)
            load_batch(ti, 0)
            yield
            lerp(WAl, Zl, 0, self.col("mu_wa"))
            lerp(G0, Zl, 1, self.col("mu_g", 0))
            lerp(G1, Zl, 2, self.col("mu_g", 1, slice(0, 32)), rows=slice(0, 32))
            self.ACT(TWA[0:64, :], WAl[0:64, :], AF.Tanh, [WAl.r], [TWA.r])
            self.CPY("dve", TWA[64:128, :], WAl[64:128, :], [WAl.r], [TWA.r])
            self.ACT(SG0[:], G0[:], AF.Sigmoid, [G0.r], [SG0.r])
            self.ACT(SG1[:], G1[:], AF.Sigmoid, [G1.r], [SG1.r])
            yield
            for bq in range(4):
                gi = ti * 4 + bq
                Zb = ZB[gi % 2]
                if bq + 1 < 4:
                    load_batch(ti, bq + 1)
                Tt = tset[0]
                R_, K_, V_, D1, D2, D3, SW, A_, kk0, ta = (Tt[n] for n in "R K V D1 D2 D3 SW A kk0 ta".split())
                sq, rkm = tsq[gi % 2], trk[0]
                c0 = 2 * bq
                for j, (X_, Dj) in enumerate(((R_, D1), (K_, D2), (V_, D3))):
                    for q in range(2):
                        mi = 8 * j + c0 + q
                        self.ACT(Dj[:, q, :], Zb[:, 2 * j + q, 0:TW], AF.Identity, [Zb.r, self.colp.r], [Dj.r], scale=self.col("mu_rkv", mi))
                        self.STT("dve", X_[:, q, :], Zb[:, 2 * j + q, 1:TW + 1], omu[:, mi:mi + 1], Dj[:, q, :], ALU.mult, ALU.add,
                                 [Zb.r, omu.r, Dj.r], [X_.r])
                bw, ba, bg = self.bank(), self.bank(), self.bank()
                for q in range(2):
                    cs = slice((c0 + q) * 128, (c0 + q + 1) * 128)
                    qs = slice(q * TW, (q + 1) * TW)
                    self.MM(ps[:, bw, qs], W2A2[0:64, cs], TWA[0:64, :], True, True, [W2A2.r, TWA.r], [PB[bw]])
                    self.MM(ps[:, ba, qs], W2A2[64:128, cs], TWA[64:128, :], True, True, [W2A2.r, TWA.r], [PB[ba]])
                for q in range(2):
                    cs = slice((c0 + q) * 128, (c0 + q + 1) * 128)
                    qs = slice(q * TW, (q + 1) * TW)
                    self.MM(ps[:, bg, qs], G2a[:, cs], SG0[:], True, False, [G2a.r, SG0.r], [PB[bg]])
                    self.MM(ps[:, bg, qs], G2b[0:32, cs], SG1[0:32, :], False, True, [G2b.r, SG1.r], [PB[bg]])
                for q in range(2):
                    qs = slice(q * TW, (q + 1) * TW)
                    self.ACT(SW[:, q, :], ps[:, bw, qs], AF.Sigmoid, [PB[bw], self.colp.r], [SW.r], bias=self.col("w0", c0 + q))
                    self.ACT(A_[:, q, :], ps[:, ba, qs], AF.Sigmoid, [PB[ba], self.colp.r], [A_.r], bias=self.col("a0", c0 + q))
                p3 = lambda b: ps[:, b, :].rearrange("p (q t) -> p q t", q=2)
                self.CPY("act", Gt[:, c0:c0 + 2, :], p3(bg), [PB[bg]], [Gt.r])
                for q in range(2):
                    P.op("dve", lambda e, o=D1[:, q, :], m=scanm[:], s_=SW[:, q, :]: e.tensor_tensor_scan(out=o, data0=m, data1=s_, initial=0.0, op0=ALU.mult, op1=ALU.add),
                         [scanm.r, SW.r], [D1.r])
                self.TT("pool", SW[:], D1[:], SW[:], ALU.subtract, [D1.r, SW.r], [SW.r])
                self.ACT(D2[:], D1[:], AF.Exp, [D1.r], [D2.r], scale=-C0)
                self.ACT(D1[:], D1[:], AF.Exp, [D1.r], [D1.r], scale=C0)
                self.ACT(SW[:], SW[:], AF.Exp, [SW.r], [SW.r], scale=-C0)
                self.CPY("pool", PC[:, c0:c0 + 2, :], D2[:].rearrange("p q (n i) -> p q n i", i=64)[:, :, :, 63], [D2.r], [PC.r])
                yield
                self.TT("dve", kk0[:], K_[:], colb("kk", c0), ALU.mult, [K_.r, self.colp.r], [kk0.r])
                self.ACT(sq[:], kk0[:], AF.Square, [kk0.r], [sq.r])
                bs = self.bank()
                for q in range(2):
                    self.MM(ps[:, bs, q * TW:(q + 1) * TW], bones_b[:], sq[:, q, :], True, True, [bones_b.r, sq.r], [PB[bs]])
                self.ACT(D3[:], p3(bs), AF.Ln, [PB[bs], tiny.r], [D3.r], bias=tiny[:, 0:1])
                self.ACT(D3[:], D3[:], AF.Exp, [D3.r], [D3.r], scale=-0.5)
                self.TT("dve", kk0[:], kk0[:], D3[:], ALU.mult, [kk0.r, D3.r], [kk0.r])
                for q in range(2):
                    self.TS("dve", ta[:, q, :], A_[:, q, :], self.col("ka", c0 + q), omk[:, c0 + q:c0 + q + 1], ALU.mult, ALU.add,
                            [A_.r, self.colp.r, omk.r], [ta.r])
                self.TT("pool", ta[:], K_[:], ta[:], ALU.mult, [K_.r, ta.r], [ta.r])
                self.TT("pool", A_[:], kk0[:], A_[:], ALU.mult, [kk0.r, A_.r], [A_.r])
                v5 = lambda ap: ap.rearrange("p q (n i) -> p q n i", i=64)
                self.TT("dve", AR[:, c0:c0 + 2, :, 1, :], v5(R_[:]), v5(D2[:]), ALU.mult, [R_.r, D2.r], [AR.r])
                for q in range(2):
                    self.STT("dve", rkm[:, q, :], R_[:, q, :], self.col("rk", c0 + q), ta[:, q, :], ALU.mult, ALU.mult,
                             [R_.r, ta.r, self.colp.r], [rkm.r])
                bb_ = self.bank()
                for q in range(2):
                    self.MM(ps[:, bb_, q * TW:(q + 1) * TW], bones_b[:], rkm[:, q, :], True, True, [bones_b.r, rkm.r], [PB[bb_]])
                self.TT("dve", bonus[:, c0:c0 + 2, :], p3(bb_), V_[:], ALU.mult, [PB[bb_], V_.r], [bonus.r])
                self.STT("dve", Af[:, c0:c0 + 2, :], kk0[:], -1.0, SW[:], ALU.mult, ALU.mult, [kk0.r, SW.r], [Af.r])
                self.CPY("pool", AR[:, c0:c0 + 2, :, 0, :], v5(Af[:, c0:c0 + 2, :]), [Af.r], [AR.r])
                self.TT("pool", Bf[:, c0:c0 + 2, :], A_[:], D1[:], ALU.mult, [A_.r, D1.r], [Bf.r])
                self.TT("dve", Kf[:, c0:c0 + 2, :], ta[:], D1[:], ALU.mult, [ta.r, D1.r], [Kf.r])
                self.CPY("act", Vb[:, c0:c0 + 2, :], V_[:], [V_.r], [Vb.r])
                yield
        def Q_gen(ti):
            t0 = ti * TW
            AR, Bf, Kf, Af, Vb, Gt, bonus, PC = (X_[ti % 2] for X_ in (ARs, Bfs, Kfs, Afs, Vbs, Gts, bonuss, PCs))
            for (src, dst) in ((Af, Atm), (Bf, Btm), (Kf, Ktm), (Vb, Vtm)):
                for blk in range(2):
                    b = self.bank()
                    psb = ps[:, b, :].bitcast(BF16)
                    for c in range(8):
                        self.TR(psb[:, c * 128:(c + 1) * 128], src[:, c, blk * 128:(blk + 1) * 128], self.ident_b[:],
                                [src.r, self.ident_b.r], [PB[b]])
                    self.CPY(self.ev(), dst[:, blk, :], psb, [PB[b]], [dst.r])
            yield
            for units in ([(0, 0), (0, 1)], [(1, 0), (1, 1)]):
                for (blk, half) in units:
                    hsl = slice(half * 8, (half + 1) * 8)
                    bks = {}
                    for par in range(2):
                        bks[par] = (self.bank(), self.bank(), self.bank())
                    for n in (2 * blk, 2 * blk + 1):
                        po = (n % 2) * 64
                        for hh in range(8):
                            h = half * 8 + hh
                            hp, ho, par, a = h // 2, (h % 2) * 64, h % 2, hh // 2
                            b_ba, b_ka, b_a0 = bks[par]
                            arv = AR[ho:ho + 64, hp, n, :, :]
                            self.MM(ps[po:po + 64, b_ba, a * 128:(a + 1) * 128].rearrange("p (a x) -> p a x", x=64),
                                    Bf[ho:ho + 64, hp, n * 64:(n + 1) * 64], arv, True, True, [Bf.r, AR.r], [PB[b_ba]])
                            self.MM(ps[po:po + 64, b_ka, a * 128:(a + 1) * 128].rearrange("p (a x) -> p a x", x=64),
                                    Kf[ho:ho + 64, hp, n * 64:(n + 1) * 64], arv, True, True, [Kf.r, AR.r], [PB[b_ka]])
                            self.MM(ps[po:po + 64, b_a0, a * 64:(a + 1) * 64], AR[ho:ho + 64, hp, n, 0, :],
                                    Bf[ho:ho + 64, hp, n * 64:(n + 1) * 64], True, True, [AR.r, Bf.r], [PB[b_a0]])
                    for par in range(2):
                        b_ba, b_ka, b_a0 = bks[par]
                        m4b = mask2[:, 0:128].unsqueeze(1).to_broadcast([128, 4, 128])
                        m4a = mask2[:, 128:192].unsqueeze(1).to_broadcast([128, 4, 64])
                        hv0 = lambda t_: t_[:, 0, hsl, :].rearrange("p (a two) x -> p two a x", two=2)[:, par]
                        hv = lambda t_: t_[:, blk, hsl, :].rearrange("p (a two) x -> p two a x", two=2)[:, par]
                        self.TT("dve", hv(BAs), ps[:, b_ba, :].rearrange("p (a x) -> p a x", x=128), m4b, ALU.mult, [PB[b_ba], mask2.r], [BAs.r])
                        self.TT("dve", hv(KAs), ps[:, b_ka, :].rearrange("p (a x) -> p a x", x=128), m4b, ALU.mult, [PB[b_ka], mask2.r], [KAs.r])
                        self.TT("dve", hv0(NA[0]), ps[:, b_a0, 0:256].rearrange("p (a x) -> p a x", x=64), m4a, ALU.mult, [PB[b_a0], mask2.r], [NA[0].r])
                    self.TT("pool", NT[0][:, 0, hsl, :], BAs[:, blk, hsl, 0:64], idn64[:].unsqueeze(1).to_broadcast([128, 8, 64]), ALU.add,
                            [BAs.r, idn64.r], [NT[0].r])
                    yield
                for lvl in range(5):
                    Ak, An = NA[lvl % 2], NA[(lvl + 1) % 2]
                    Bn = NB[(lvl + 1) % 2]
                    Tk, Tn = NT[lvl % 2], NT[(lvl + 1) % 2]
                    for (blk, half) in units:
                        hsl = slice(half * 8, (half + 1) * 8)
                        bA, bB, bT = self.bank(), self.bank(), self.bank()
                        pA = ps[:, bA, :].rearrange("p (h x) -> p h x", x=64)
                        pB = ps[:, bB, :].rearrange("p (h x) -> p h x", x=64)
                        pT = ps[:, bT, :].rearrange("p (h x) -> p h x", x=64)

                        def Bk_ap(po, h):
                            if lvl == 0:
                                return BAs[po:po + 64, blk, h, 0:64]
                            return NB[lvl % 2][po:po + 64, 0, h, :]
                        Bk_r = BAs.r if lvl == 0 else NB[lvl % 2].r
                        for po in (0, 64):
                            for hh in range(8):
                                h = half * 8 + hh
                                self.MM(pA[po:po + 64, hh, :], Bk_ap(po, h), Ak[po:po + 64, 0, h, :], True, True, [Bk_r, Ak.r], [PB[bA]])
                                if lvl < 4:
                                    self.MM(pB[po:po + 64, hh, :], Ak[po:po + 64, 0, h, :], Bk_ap(po, h), True, True, [Bk_r, Ak.r], [PB[bB]])
                        self.CPY("act", An[:, 0, hsl, :], pA, [PB[bA]], [An.r])
                        if lvl < 4:
                            self.CPY("act", Bn[:, 0, hsl, :], pB, [PB[bB]], [Bn.r])
                        for po in (0, 64):
                            for hh in range(8):
                                h = half * 8 + hh
                                self.MM(pT[po:po + 64, hh, :], An[po:po + 64, 0, h, :], Tk[po:po + 64, 0, h, :], True, True, [An.r, Tk.r], [PB[bT]])
                        self.TT("dve", Tn[:, 0, hsl, :], pT, Tk[:, 0, hsl, :], ALU.add, [PB[bT], Tk.r], [Tn.r])
                        yield
                TTf = NT[1]
                for (blk, half) in units:
                    hsl = slice(half * 8, (half + 1) * 8)
                    bP = (self.bank(), self.bank())
                    bV = self.bank()
                    pV = ps[:, bV, :].rearrange("p (h x) -> p h x", x=64)
                    for n in (2 * blk, 2 * blk + 1):
                        po = (n % 2) * 64
                        pP = ps[:, bP[n % 2], :].rearrange("p (q a x) -> p q a x", q=2, x=64)
                        for hh in range(8):
                            h = half * 8 + hh
                            ho = (h % 2) * 64
                            fs = slice(h * 64, (h + 1) * 64)
                            self.MM(pP[ho:ho + 64, 0, hh // 2, :], Atm[po:po + 64, blk, fs], TTf[po:po + 64, 0, h, :], True, True,
                                    [Atm.r, TTf.r], [PB[bP[n % 2]]])
                            self.MM(pV[po:po + 64, hh, :], KAs[po:po + 64, blk, h, 0:64], Vtm[po:po + 64, blk, fs], True, True,
                                    [KAs.r, Vtm.r], [PB[bV]])
                            self.MM(pP[ho:ho + 64, 1, hh // 2, :], Ktm[po:po + 64, blk, fs], Vtm[po:po + 64, blk, fs], True, True,
                                    [Ktm.r, Vtm.r], [PB[bP[n % 2]]])
                    for n in (2 * blk, 2 * blk + 1):
                        pP = ps[:, bP[n % 2], :].rearrange("p (q a x) -> p q a x", q=2, x=64)
                        self.CPY("act", M1T[:, half * 4:(half + 1) * 4, n, :], pP[:, 0], [PB[bP[n % 2]]], [M1T.r])
                        self.CPY("act", KVs[:, half * 4:(half + 1) * 4, n, :], pP[:, 1], [PB[bP[n % 2]]], [KVs.r])
                    self.CPY("act", AKVs[:, blk, hsl, :], pV, [PB[bV]], [AKVs.r])
                    bU = self.bank()
                    pU = ps[:, bU, :].rearrange("p (h x) -> p h x", x=64)
                    for po in (0, 64):
                        for hh in range(8):
                            h = half * 8 + hh
                            self.MM(pU[po:po + 64, hh, :], TTf[po:po + 64, 0, h, :], AKVs[po:po + 64, blk, h, :], True, True,
                                    [TTf.r, AKVs.r], [PB[bU]])
                    self.CPY("act", U2s[:, blk, hsl, :], pU, [PB[bU]], [U2s.r])
                    yield
            for n in range(4):
                blk, po = n // 2, (n % 2) * 64
                Hbf, Hnx = Hbfs[(ti * 4 + n) % 2], Hbfs[(ti * 4 + n + 1) % 2]
                bU = bank2()
                pU4 = ps[:, bU:bU + 2, :].rearrange("p a (h x) -> p a h x", x=64)
                for h in range(16):
                    hp, ho = h // 2, (h % 2) * 64
                    self.MM(pU4[po:po + 64, h % 2, hp, :], M1T[ho:ho + 64, hp, n, :], Hbf[ho:ho + 64, hp, :], True, True,
                            [M1T.r, Hbf.r], [PB[bU + (h % 2)]])
                hq = lambda t_: t_[po:po + 64, blk, :, :].rearrange("p (hp two) v -> p two hp v", two=2)
                self.TT("dve", hq(Utm), pU4[po:po + 64], hq(U2s), ALU.add, [PB[bU], PB[bU + 1], U2s.r], [Utm.r])
                bH = self.bank()
                pH = ps[:, bH, :].rearrange("p (h x) -> p h x", x=64)
                for h in range(16):
                    hp, ho = h // 2, (h % 2) * 64
                    self.MM(pH[ho:ho + 64, hp, :], Btm[po:po + 64, blk, h * 64:(h + 1) * 64], Utm[po:po + 64, blk, h, :], True, True,
                            [Btm.r, Utm.r], [PB[bH]])
                pcb = PC[:, :, n:n + 1].to_broadcast([128, 8, 64])
                self.TT("pool", HK[:], H32[:], KVs[:, :, n, :], ALU.add, [H32.r, KVs.r], [HK.r])
                self.TT("dve", T1[:], pH, HK[:], ALU.add, [PB[bH], HK.r], [T1.r])
                self.TT("dve", Hnx[:], T1[:], pcb, ALU.mult, [T1.r, PC.r], [Hnx.r])
                self.TT("pool", H32[:], T1[:], pcb, ALU.mult, [T1.r, PC.r], [H32.r])
                bY1, bY2 = self.bank(), self.bank()
                pY1 = ps[:, bY1, :].rearrange("p (h x) -> p h x", x=64)
                pY2 = ps[:, bY2, :].rearrange("p (h x) -> p h x", x=64)
                for h in range(16):
                    hp, ho = h // 2, (h % 2) * 64
                    self.MM(pY1[ho:ho + 64, hp, :], Hbf[ho:ho + 64, hp, :], AR[ho:ho + 64, hp, n, 1, :], True, True, [Hbf.r, AR.r], [PB[bY1]])
                for h in range(16):
                    hp, ho = h // 2, (h % 2) * 64
                    yo = pY2[ho:ho + 64, hp, :]
                    self.MM(yo, Utm[po:po + 64, blk, h, :], BAs[po:po + 64, blk, h, 64:128], True, False, [Utm.r, BAs.r], [PB[bY2]])
                    self.MM(yo, Vtm[po:po + 64, blk, h * 64:(h + 1) * 64], KAs[po:po + 64, blk, h, 64:128], False, True, [Vtm.r, KAs.r], [PB[bY2]])
                self.CPY("act", HK[:], pY1, [PB[bY1]], [HK.r])
                self.TT("dve", Yf[:, :, n * 64:(n + 1) * 64], pY2, HK[:], ALU.add, [PB[bY2], HK.r], [Yf.r])
                yield
            for bq in range(4):
                yc, rst, yn, ysq = Hy, T1y, ynb, tsq[1]
                c0 = 2 * bq
                p3 = lambda b: ps[:, b, :].rearrange("p (q t) -> p q t", q=2)
                bm = self.bank()
                for q in range(2):
                    self.MM(ps[:, bm, q * TW:(q + 1) * TW], bones[:], Yf[:, c0 + q, :], True, True, [bones.r, Yf.r], [PB[bm]])
                self.STT("dve", yc[:], p3(bm), -1.0 / 64, Yf[:, c0:c0 + 2, :], ALU.mult, ALU.add, [PB[bm], Yf.r], [yc.r])
                self.ACT(ysq[:], yc[:], AF.Square, [yc.r], [ysq.r])
                bv_ = self.bank()
                for q in range(2):
                    self.MM(ps[:, bv_, q * TW:(q + 1) * TW], bones_b[:], ysq[:, q, :], True, True, [bones_b.r, ysq.r], [PB[bv_]])
                self.ACT(rst[:], p3(bv_), AF.Ln, [PB[bv_], self.epsc.r], [rst.r], bias=self.epsc[:, 1:2], scale=1.0 / 64)
                self.ACT(rst[:], rst[:], AF.Exp, [rst.r], [rst.r], scale=-0.5)
                self.TT("pool", yn[:], yc[:], rst[:], ALU.mult, [yc.r, rst.r], [yn.r])
                for q in range(2):
                    self.ACT(yn[:, q, :], yn[:, q, :], AF.Identity, [yn.r, self.colp.r], [yn.r], bias=self.col("lnb", c0 + q), scale=self.col("lnw", c0 + q))
                self.TT("dve", yn[:], yn[:], bonus[:, c0:c0 + 2, :], ALU.add, [yn.r, bonus.r], [yn.r])
                YO = YOb[bq % 2]
                self.TT("dve", YO[:], yn[:], Gt[:, c0:c0 + 2, :], ALU.mult, [yn.r, Gt.r], [YO.r])
                self.DMA("sp", self.YRT[c0 * 128:(c0 + 2) * 128, t0:t0 + TW].rearrange("(c p) t -> p c t", p=128), YO[:], [YO.r], [Region()])
                yield
        if os.environ.get('KD_SCHED', '1') == '1':
            P.defer = True
        for _ in P_gen(0):
            pass
        for ti in range(ntile):
            p = P_gen(ti + 1) if ti + 1 < ntile else None
            for _ in Q_gen(ti):
                if p is not None:
                    try:
                        next(p)
                    except StopIteration:
                        p = None
            if p is not None:
                for _ in p:
                    pass
        if P.defer:
            P.schedule()
        self.P.barrier()
        self.P.flush()


KB.phaseD = _phaseD


TE = 256


def _phaseE(self, hT2):
    ps, PB = self.ps, self.PB
    with ExitStack() as st:
        sb = lambda shape, dt, name: self.sb(st, shape, dt, name)
        wao = sb([128, 4, D], BF16, "wao")
        wro = sb([128, 8, D], BF16, "wro")
        wo = sb([128, 8, D], BF16, "wo")
        with ExitStack() as st2:
            wstg = [self.sb(st2, [128, 4, D], F32, "wstg") for _ in range(2)]
            i = 0
            for dst, src, nk in ((wao, self.w_att_out, 4), (wro, self.w_rwkv_out, 8), (wo, self.w_o, 8)):
                for k0 in range(0, nk, 4):
                    wsg = wstg[i % 2]
                    i += 1
                    self.DMA("sp", wsg[:], src[k0 * 128:(k0 + 4) * 128, :].rearrange("(k p) n -> p k n", p=128), (), [wsg.r])
                    self.CPY("pool", dst[:, k0:k0 + 4, :], wsg[:], [wsg.r], [dst.r])
            self.P.barrier()
            self.P.flush()
        if os.environ.get('KD_SCHED', '1') == '1':
            self.P.defer = True
        yat = [sb([128, 4, TE], BF16, "yat") for _ in range(2)]
        yrt = [sb([128, 8, TE], BF16, "yrt") for _ in range(2)]
        gat = [sb([128, 8, TE], BF16, "gat") for _ in range(2)]
        grt = [sb([128, 8, TE], BF16, "grt") for _ in range(2)]
        ga = [sb([128, TE], F32, "ga") for _ in range(2)]
        gr = [sb([128, TE], F32, "gr") for _ in range(2)]
        m1 = [sb([128, TE], F32, "m1") for _ in range(2)]
        m2 = [sb([128, TE], F32, "m2") for _ in range(2)]
        mix = [sb([128, 8, TE], BF16, "mix") for _ in range(2)]
        xt = [sb([128, D], F32, "xt") for _ in range(2)]
        x1t = [sb([128, D], F32, "x1t") for _ in range(2)]
        tz = [sb([128, 512], F32, "tz") for _ in range(2)]
        xs = [sb([128, D], F32, "xs") for _ in range(4)]
        junk = sb([128, D], F32, "junk")
        ssq = [sb([128, 1], F32, "ssq") for _ in range(2)]
        rs = [sb([128, 1], F32, "rs") for _ in range(2)]
        k = 0
        bi = 0
        for ti in range(S // TE):
            t0 = ti * TE
            ya_, yr_, ga_l, gr_l, mx = yat[ti % 2], yrt[ti % 2], gat[ti % 2], grt[ti % 2], mix[ti % 2]
            tsl = slice(t0, t0 + TE)
            self.DMA("sp", ya_[:], self.YAT[:, tsl].rearrange("(k p) t -> p k t", p=128), (), [ya_.r])
            self.DMA("sp", yr_[:], self.YRT[:, tsl].rearrange("(k p) t -> p k t", p=128), (), [yr_.r])
            self.DMA("pool", ga_l[:], self.PF[GATE0:GATE0 + D, tsl].rearrange("(k p) t -> p k t", p=128), (), [ga_l.r])
            self.DMA("pool", gr_l[:], self.PF[GATE0 + D:GATE0 + 2 * D, tsl].rearrange("(k p) t -> p k t", p=128), (), [gr_l.r])
            for oc in range(8):
                ocs = slice(oc * 128, (oc + 1) * 128)
                b = self.bank()
                for kc in range(4):
                    self.MM(ps[:, b, 0:TE], wao[:, kc, ocs], ya_[:, kc, :], kc == 0, kc == 3, [wao.r, ya_.r], [PB[b]])
                for kc in range(8):
                    self.MM(ps[:, b, TE:2 * TE], wro[:, kc, ocs], yr_[:, kc, :], kc == 0, kc == 7, [wro.r, yr_.r], [PB[b]])
                g1, g2_, a1, a2_ = ga[k % 2], gr[k % 2], m1[k % 2], m2[k % 2]
                k += 1
                self.ACT(g1[:], ga_l[:, oc, :], AF.Sigmoid, [ga_l.r, self.colp.r], [g1.r], bias=self.col("bgate", oc))
                self.ACT(g2_[:], gr_l[:, oc, :], AF.Sigmoid, [gr_l.r, self.colp.r], [g2_.r], bias=self.col("bgate", 8 + oc))
                self.TT("dve", a1[:], ps[:, b, 0:TE], g1[:], ALU.mult, [PB[b], g1.r], [a1.r])
                self.TT("dve", a2_[:], ps[:, b, TE:2 * TE], g2_[:], ALU.mult, [PB[b], g2_.r], [a2_.r])
                self.TT("pool", mx[:, oc, :], a1[:], a2_[:], ALU.add, [a1.r, a2_.r], [mx.r])
            pair = []
            for sbk in range(2):
                r0 = t0 + sbk * 128
                x_, x1_ = xt[bi % 2], x1t[bi % 2]
                self.DMA("sp", x_[:], self.x[r0:r0 + 128, :], (), [x_.r])
                for half in range(2):
                    hs_ = slice(half * 512, (half + 1) * 512)
                    b = self.bank()
                    for kc in range(8):
                        self.MM(ps[:, b, :], mx[:, kc, sbk * 128:(sbk + 1) * 128], wo[:, kc, hs_], kc == 0, kc == 7, [mx.r, wo.r], [PB[b]])
                    tz_ = tz[half]
                    self.TT("dve", tz_[:], ps[:, b, :], self.GT1[:, hs_], ALU.mult, [PB[b], self.GT1.r], [tz_.r])
                    self.TT("pool", x1_[:, hs_], tz_[:], x_[:, hs_], ALU.add, [tz_.r, x_.r], [x1_.r])
                self.DMA("sp", self.X1[r0:r0 + 128, :], x1_[:], [x1_.r], [Region()])
                xs_ = xs[bi % 4]
                self.norm_block(x1_, xs_, junk, ssq[bi % 2], rs[bi % 2])
                pair.append(xs_)
                bi += 1
            self.transpose_group(pair, ti, self.A2c, self.S2c, hT2)
        if self.P.defer:
            self.P.schedule()
        self.P.barrier()
        self.P.flush()


def _phaseF1(self, hT2):
    ps, PB = self.ps, self.PB
    NF = DFF // 128
    with ExitStack() as st:
        sb = lambda shape, dt, name: self.sb(st, shape, dt, name)
        wf = [sb([128, 8, 256], F32, "wf") for _ in range(2)]
        wb = [sb([128, 8, 256], BF16, "wb") for _ in range(2)]
        NBUF = 4
        UG = [sb([128, 514], F32, "UG") for _ in range(NBUF)]
        UV = [sb([128, 514], F32, "UV") for _ in range(NBUF)]
        cg = [sb([128, 512], F32, "cg") for _ in range(NBUF)]
        cv = [sb([128, 512], F32, "cv") for _ in range(NBUF)]
        sg = [sb([128, 512], F32, "sg") for _ in range(NBUF)]
        ao = [sb([128, 512], BF16, "ao") for _ in range(NBUF)]
        tpl = [sb([128, 512], F32, "tpl") for _ in range(NBUF)]
        tpl2 = [sb([128, 512], F32, "tpl2") for _ in range(NBUF)]
        if os.environ.get('KD_SCHED', '1') == '1':
            self.P.defer = True
        UGh = [Region("ugh") for _ in range(NBUF)]
        UVh = [Region("uvh") for _ in range(NBUF)]
        k = 0
        pending = None
        for f in range(NF):
            f_, b_ = wf[f % 2], wb[f % 2]
            src = self.w_up.rearrange("(k p) n -> p k n", p=128)
            self.DMA("sp", f_[:, :, 0:128], src[:, :, f * 128:(f + 1) * 128], (), [f_.r])
            self.DMA("sp", f_[:, :, 128:256], src[:, :, DFF + f * 128:DFF + (f + 1) * 128], (), [f_.r])
            self.CPY("dve", b_[:, 0:4, :], f_[:, 0:4, :], [f_.r], [b_.r])
            self.CPY("dve", b_[:, 4:8, :], f_[:, 4:8, :], [f_.r], [b_.r])
            for tt in range(8):
                t0 = tt * 512
                ug, uv = UG[k % NBUF], UV[k % NBUF]
                ugp, uvp = UG[(k - 1) % NBUF], UV[(k - 1) % NBUF]
                cg_, cv_, sg_, ao_ = cg[k % NBUF], cv[k % NBUF], sg[k % NBUF], ao[k % NBUF]
                k += 1
                bg, bv = self.bank(), self.bank()
                for kc in range(8):
                    self.MM(ps[:, bg, :], b_[:, kc, 0:128], hT2[:, kc, t0:t0 + 512], kc == 0, kc == 7, [b_.r, hT2.r], [PB[bg]])
                for kc in range(8):
                    self.MM(ps[:, bv, :], b_[:, kc, 128:256], hT2[:, kc, t0:t0 + 512], kc == 0, kc == 7, [b_.r, hT2.r], [PB[bv]])
                ugh, uvh = UGh[(k - 1) % NBUF], UVh[(k - 1) % NBUF]
                if tt == 0:
                    self.MSET("pool", ug[:, 0:2], 0.0, [ugh])
                    self.MSET("pool", uv[:, 0:2], 0.0, [uvh])
                else:
                    self.CPY("dve", ug[:, 0:2], ugp[:, 512:514], [ugp.r], [ugh])
                    self.CPY("dve", uv[:, 0:2], uvp[:, 512:514], [uvp.r], [uvh])
                self.CPY("act", ug[:, 2:514], ps[:, bg, :], [PB[bg]], [ug.r])
                self.CPY("act", uv[:, 2:514], ps[:, bv, :], [PB[bv]], [uv.r])
                self.TS("dve", cg_[:], ug[:, 2:514], self.col("cw2", f), self.col("cb", f), ALU.mult, ALU.add, [ug.r, self.colp.r], [cg_.r])
                self.STT("dve", cg_[:], ug[:, 1:513], self.col("cw1", f), cg_[:], ALU.mult, ALU.add, [ug.r, ugh, cg_.r, self.colp.r], [cg_.r])
                self.STT("dve", cg_[:], ug[:, 0:512], self.col("cw0", f), cg_[:], ALU.mult, ALU.add, [ug.r, ugh, cg_.r, self.colp.r], [cg_.r])
                jf = NF + f
                tp_ = tpl[k % NBUF]
                tq_ = tpl2[k % NBUF]
                self.ACT(cv_[:], uv[:, 2:514], AF.Identity, [uv.r, self.colp.r], [cv_.r], bias=self.col("cb", jf), scale=self.col("cw2", jf))
                self.ACT(tp_[:], uv[:, 1:513], AF.Identity, [uv.r, uvh, self.colp.r], [tp_.r], scale=self.col("cw1", jf))
                self.ACT(tq_[:], uv[:, 0:512], AF.Identity, [uv.r, uvh, self.colp.r], [tq_.r], scale=self.col("cw0", jf))
                self.TT("pool", tp_[:], tp_[:], tq_[:], ALU.add, [tp_.r, tq_.r], [tp_.r])
                self.TT("pool", cv_[:], cv_[:], tp_[:], ALU.add, [cv_.r, tp_.r], [cv_.r])
                if pending is not None:
                    pending()

                def tail(cg_=cg_, cv_=cv_, sg_=sg_, ao_=ao_, f=f, t0=t0):
                    self.ACT(sg_[:], cg_[:], AF.Silu, [cg_.r], [sg_.r])
                    self.TT("dve", ao_[:], sg_[:], cv_[:], ALU.mult, [sg_.r, cv_.r], [ao_.r])
                    self.DMA("sp", self.ACTS[f * 128:(f + 1) * 128, t0:t0 + 512], ao_[:], [ao_.r], [Region()])
                pending = tail
        pending()
        if self.P.defer:
            self.P.schedule()
        self.P.barrier()
        self.P.flush()


def _phaseF2(self):
    ps, PB = self.ps, self.PB
    NF = DFF // 128
    with ExitStack() as st:
        sb = lambda shape, dt, name: self.sb(st, shape, dt, name)
        wd = sb([128, NF, D], BF16, "wd")
        wstg = [sb([128, 2, D], F32, "wstg") for _ in range(2)]
        for i, k0 in enumerate(range(0, NF, 2)):
            wsg = wstg[i % 2]
            self.DMA("sp", wsg[:], self.w_down[k0 * 128:(k0 + 2) * 128, :].rearrange("(k p) n -> p k n", p=128), (), [wsg.r])
            self.CPY("pool", wd[:, k0:k0 + 2, :], wsg[:], [wsg.r], [wd.r])
        if os.environ.get('KD_SCHED', '1') == '1':
            self.P.defer = True
        NFW = sb([128, D], F32, "nfw")
        self.DMA("sp", NFW[:], self.norm_f_w.partition_broadcast(128), (), [NFW.r])
        at = [sb([128, NF, 512], BF16, "at") for _ in range(2)]
        x1t = [sb([128, D], F32, "x1t") for _ in range(3)]
        x2t = [sb([128, D], F32, "x2t") for _ in range(3)]
        xs = [sb([128, D], F32, "xs") for _ in range(3)]
        ot = [sb([128, D], F32, "ot") for _ in range(3)]
        tz = [sb([128, 512], F32, "tz") for _ in range(4)]
        junk = sb([128, D], F32, "junk")
        ssq = [sb([128, 1], F32, "ssq") for _ in range(2)]
        rs = [sb([128, 1], F32, "rs") for _ in range(2)]
        outs = []
        bi = 0
        for tt in range(8):
            t0 = tt * 512
            a_ = at[tt % 2]
            self.DMA("pool", a_[:], self.ACTS[:, t0:t0 + 512].rearrange("(k p) t -> p k t", p=128), (), [a_.r])
            for sbk in range(4):
                r0 = t0 + sbk * 128
                x1_, x2_, xs_, o_ = x1t[bi % 3], x2t[bi % 3], xs[bi % 3], ot[bi % 3]
                self.DMA("sp", x1_[:], self.X1[r0:r0 + 128, :], (), [x1_.r])
                for half in range(2):
                    hs_ = slice(half * 512, (half + 1) * 512)
                    b = self.bank()
                    for kc in range(NF):
                        self.MM(ps[:, b, :], a_[:, kc, sbk * 128:(sbk + 1) * 128], wd[:, kc, hs_], kc == 0, kc == NF - 1, [a_.r, wd.r], [PB[b]])
                    tz_ = tz[(2 * bi + half) % 4]
                    self.TT("dve", tz_[:], ps[:, b, :], self.GT2[:, hs_], ALU.mult, [PB[b], self.GT2.r], [tz_.r])
                    self.TT("pool", x2_[:, hs_], tz_[:], x1_[:, hs_], ALU.add, [tz_.r, x1_.r], [x2_.r])
                self.norm_block(x2_, xs_, junk, ssq[bi % 2], rs[bi % 2])
                self.TT("dve", o_[:], xs_[:], NFW[:], ALU.mult, [xs_.r, NFW.r], [o_.r])
                rg = Region()
                outs.append(rg)
                self.DMA("sp", self.out[r0:r0 + 128, :], o_[:], [o_.r], [rg])
                bi += 1
        if self.P.defer:
            self.P.schedule()
        self.P.finish(outs)
        self.P.barrier()
        self.P.flush()


KB.phaseE = _phaseE
KB.phaseF1 = _phaseF1
KB.phaseF2 = _phaseF2
```

```python
import bisect
import os
from contextlib import ExitStack
import numpy as np
import ml_dtypes
import concourse.bass as bass
import concourse.mybir as mybir
from concourse.bass_utils import run_bass_kernel_spmd

F32 = mybir.dt.float32
BF16 = mybir.dt.bfloat16
ALU = mybir.AluOpType
AF = mybir.ActivationFunctionType
AX = mybir.AxisListType

ENGS = ("pe", "act", "dve", "pool", "sp")
NDMA_SEM = 8


class Region:
    __slots__ = ("name", "w", "r")

    def __init__(self, name=""):
        self.name = name
        self.w = None
        self.r = []


class Op:
    __slots__ = ("eng", "idx", "fn", "inc", "token", "is_dma", "tag")

    def __init__(self, eng, idx, fn, is_dma=False):
        self.eng = eng
        self.idx = idx
        self.fn = fn
        self.inc = None
        self.token = None
        self.is_dma = is_dma
        self.tag = None
        if os.environ.get("KD_TAG"):
            import sys
            f = sys._getframe(2)
            while f is not None and f.f_code.co_name in ("op", "dma", "MM", "TR", "ACT", "TT", "TS", "STT", "CPY", "MSET", "DMA", "lerp", "diag"):
                f = f.f_back
            self.tag = f.f_lineno if f is not None else -1


class Prog:
    def __init__(self, nc):
        self.nc = nc
        self.stream = {e: [] for e in ENGS}
        self.nops = {e: 0 for e in ENGS}
        self.cnt = {e: 0 for e in ENGS}
        self.fin = {e: ([], []) for e in ENGS}
        self.last = {e: None for e in ENGS}
        self.known = {e: {} for e in ENGS}
        self.dma_n = {e: 0 for e in ENGS}
        self.sems = {}
        self._ctx = []
        self.defer = False
        self.pend = []

    def schedule(self):
        import heapq
        pend, self.pend = self.pend, []
        self.defer = False
        n = len(pend)
        preds = [None] * n
        lastw, readers = {}, {}
        for i, (kind, eng, fn, reads, writes, cost, kw) in enumerate(pend):
            p = set()
            for R in reads:
                w = lastw.get(id(R))
                if w is not None:
                    p.add(w)
            for W in writes:
                w = lastw.get(id(W))
                if w is not None:
                    p.add(w)
                p.update(readers.get(id(W), ()))
            p.discard(i)
            preds[i] = p
            for R in reads:
                readers.setdefault(id(R), []).append(i)
            for W in writes:
                lastw[id(W)] = i
                readers[id(W)] = []
        succs = [[] for _ in range(n)]
        npred = [len(p) for p in preds]
        for i, p in enumerate(preds):
            for j in p:
                succs[j].append(i)
        DMA_LAT = 2.5
        rank = [0.0] * n
        if os.environ.get("KD_PRIO", "0") == "1":
            for i in range(n - 1, -1, -1):
                c_ = DMA_LAT if pend[i][0] == "dma" else pend[i][5]
                rank[i] = c_ + max([rank[j] for j in succs[i]], default=0.0)
        ready_t = [0.0] * n
        heaps = {e: [] for e in ENGS}
        for i in range(n):
            if npred[i] == 0:
                heapq.heappush(heaps[pend[i][1]], (0.0, i))
        free = {e: 0.0 for e in ENGS}
        order = []
        done = 0
        while done < n:
            best, be = None, None
            for e in ENGS:
                h = heaps[e]
                if h:
                    st = max(free[e], h[0][0])
                    if best is None or (st, h[0][1]) < best:
                        best, be = (st, h[0][1]), e
            assert be is not None, "scheduler deadlock"
            st = best[0]
            h = heaps[be]
            cand = []
            while h and h[0][0] <= st:
                cand.append(heapq.heappop(h))
            cand.sort(key=lambda x: (-rank[x[1]], x[1]))
            i = cand[0][1]
            for c in cand[1:]:
                heapq.heappush(h, c)
            kind, eng, fn, reads, writes, cost, kw = pend[i]
            fin = st + cost
            free[be] = fin
            if kind == "dma":
                fin = st + DMA_LAT
            order.append(i)
            done += 1
            for j in succs[i]:
                ready_t[j] = max(ready_t[j], fin)
                npred[j] -= 1
                if npred[j] == 0:
                    heapq.heappush(heaps[pend[j][1]], (ready_t[j], j))
        if os.environ.get("KD_SCHEDSTAT"):
            busy = {e: 0.0 for e in ENGS}
            for rec in pend:
                busy[rec[1]] += rec[5]
            rk = [0.0] * n
            for i in range(n - 1, -1, -1):
                c_ = DMA_LAT if pend[i][0] == "dma" else pend[i][5]
                rk[i] = c_ + max([rk[j] for j in succs[i]], default=0.0)
            print("SCHED n=%d makespan=%.1fus critpath=%.1fus busy=%s" % (n, max(free.values()), max(rk), {e: round(v, 1) for e, v in busy.items()}), flush=True)
        for i in order:
            kind, eng, fn, reads, writes, cost, kw = pend[i]
            if kind == "op":
                self.op(eng, fn, reads, writes)
            else:
                self.dma(eng, fn[0], fn[1], reads, writes, **kw)

    def alloc_sems(self, stack):
        for e in ENGS:
            self.sems[("c", e)] = stack.enter_context(self.nc.semaphore("c_" + e))
        for e in ("sp", "pool", "act"):
            for k in range(NDMA_SEM):
                self.sems[("d", e, k)] = stack.enter_context(self.nc.semaphore("d_%s%d" % (e, k)))

    def _token(self, op):
        if op.token is not None:
            return op.token
        e = op.eng
        idxs, vals = self.fin[e]
        p = bisect.bisect_left(idxs, op.idx)
        if p < len(idxs):
            return (("c", e), vals[p])
        L = self.last[e]
        assert L is not None and L.idx >= op.idx and not L.is_dma
        self.cnt[e] += 1
        L.inc = (("c", e), 1)
        L.token = (("c", e), self.cnt[e])
        idxs.append(L.idx)
        vals.append(self.cnt[e])
        return L.token

    def _wait(self, eng, tok):
        key, val = tok
        if self.known[eng].get(key, 0) >= val:
            return
        self.known[eng][key] = val
        self.stream[eng].append(("wait", key, val))

    def _deps(self, eng, reads, writes):
        deps = []
        for R in reads:
            if R.w is not None:
                deps.append(R.w)
        for W in writes:
            if W.w is not None:
                deps.append(W.w)
            deps.extend(W.r)
        for d in deps:
            if eng == "pe" and d.eng == "pe" and not d.is_dma:
                continue
            self._wait(eng, self._token(d))

    def _record(self, op, reads, writes):
        for R in reads:
            if not op.is_dma:
                R.r = [x for x in R.r if x.is_dma or x.eng != op.eng]
            R.r.append(op)
        for W in writes:
            W.w = op
            W.r = []

    def op(self, eng, fn, reads=(), writes=(), cost=0.5):
        if getattr(self, "budget", None) is not None:
            if self.budget <= 0:
                return None
            self.budget -= 1
        if self.defer:
            self.pend.append(("op", eng, fn, tuple(reads), tuple(writes), cost, None))
            return None
        self._deps(eng, reads, writes)
        o = Op(eng, self.nops[eng], fn)
        self.nops[eng] += 1
        self.stream[eng].append(o)
        self.last[eng] = o
        self._record(o, reads, writes)
        return o

    def dma(self, q, out, in_, reads=(), writes=(), **kw):
        if self.defer:
            self.pend.append(("dma", q, (out, in_), tuple(reads), tuple(writes), 0.06, kw))
            return None
        self._deps(q, reads, writes)
        n = self.dma_n[q]
        self.dma_n[q] += 1
        k, rnd = n % NDMA_SEM, n // NDMA_SEM
        key = ("d", q, k)
        if rnd > 0:
            self._wait(q, (key, 16 * rnd))
        o = Op(q, self.nops[q], lambda e: e.dma_start(out=out, in_=in_, **kw), is_dma=True)
        self.nops[q] += 1
        o.inc = (key, 16)
        o.token = (key, 16 * (rnd + 1))
        self.stream[q].append(o)
        self._record(o, reads, writes)
        return o

    def finish(self, regions):
        for R in regions:
            if R.w is not None:
                self._wait("sp", self._token(R.w))
            for x in R.r:
                self._wait("sp", self._token(x))

    def emit(self, block):
        nc = self.nc
        handles = {"pe": block.tensor, "act": block.scalar, "dve": block.vector,
                   "pool": block.gpsimd, "sp": block.sync}

        def make(e):
            def body(eng):
                for it in self.stream[e]:
                    if isinstance(it, tuple):
                        eng.wait_ge(self.sems[it[1]], it[2])
                    else:
                        ins = it.fn(eng)
                        if it.tag is not None:
                            try:
                                print("TAG", ins.ins.name, it.tag, flush=True)
                            except Exception as ex:
                                print("TAGERR", ex)
                        if it.inc is not None:
                            ins.then_inc(self.sems[it.inc[0]], it.inc[1])
            return body
        for e in ENGS:
            if self.stream[e]:
                handles[e](make(e))

    def barrier(self):
        toks = []
        for e in ENGS:
            L = self.last[e]
            if L is not None:
                toks.append(self._token(L))
        for q in ("sp", "pool", "act"):
            n = self.dma_n[q]
            for k in range(NDMA_SEM):
                cnt = (n - k + NDMA_SEM - 1) // NDMA_SEM
                if cnt > 0:
                    toks.append((("d", q, k), 16 * cnt))
        for e in ENGS:
            for t in toks:
                self._wait(e, t)

    def flush(self):
        with self.nc.Block() as block:
            self.emit(block)
        self.stream = {e: [] for e in ENGS}


S = 4096
D = 1024
NIN = 10016
DFF = 2816
RMS_EPS = 1e-6
GN_EPS = 64e-5
ATT_D = (1, 4, 16)
RW0 = 4608
GATE0 = 7968
CP = {}
_o = 0
for _n, _w in (("bgate", 16), ("mu_rkv", 24), ("mu_wa", 1), ("mu_g", 2), ("w0", 8), ("a0", 8), ("kk", 8),
               ("ka", 8), ("rk", 8), ("lnw", 8), ("lnb", 8), ("cw0", 44), ("cw1", 44), ("cw2", 44), ("cb", 44)):
    CP[_n] = _o
    _o += _w
NCOL = _o


class T:
    def __init__(self, t, name):
        self.t = t
        self.r = Region(name)
        self.rq = [[Region(name), Region(name)], [Region(name), Region(name)]]

    def __getitem__(self, k):
        return self.t[k]


class KB:
    def __init__(self, nc, debug=False):
        self.nc = nc
        self.P = Prog(nc)
        self.debug = debug
        self.bank_i = 0
        self.ev_i = 0
        self.uid = 0

    def sb(self, st, shape, dt, name=None):
        self.uid += 1
        name = "%s_%d" % (name or "t", self.uid)
        if not hasattr(self, "names"):
            self.names = {}
        self.names.setdefault((name.rsplit("_", 1)[0]), []).append(name)
        return T(st.enter_context(self.nc.sbuf_tensor(name, list(shape), dt)), name)

    def bank(self):
        b = self.bank_i
        self.bank_i = (b + 1) % 8
        return b

    def ev(self):
        self.ev_i ^= 1
        return "act" if self.ev_i else "dve"

    @staticmethod
    def fsz(ap):
        n = 1
        for d_ in ap.shape[1:]:
            n *= d_
        return n

    def MM(self, out, lhsT, rhs, start, stop, rd, wr):
        c = 0.035 + self.fsz(out) / 2400.0 * (4 if lhsT.dtype == F32 else 1)
        self.P.op("pe", lambda e: e.matmul(out, lhsT=lhsT, rhs=rhs, start=start, stop=stop), rd, wr, cost=c)

    def TR(self, out, in_, ident, rd, wr):
        self.P.op("pe", lambda e: e.transpose(out, in_, ident), rd, wr, cost=0.035 + self.fsz(out) / 2400.0 * (4 if in_.dtype == F32 else 1))

    def ACT(self, out, in_, func, rd, wr, bias=None, scale=None, accum=None):
        kw = {}
        if bias is not None:
            kw["bias"] = bias
        if scale is not None:
            kw["scale"] = scale
        if accum is not None:
            kw["accum_out"] = accum
        self.P.op("act", lambda e: e.activation(out=out, in_=in_, func=func, **kw), rd, wr, cost=0.2 + self.fsz(out) / 1200.0)

    def TT(self, eng, out, in0, in1, op, rd, wr):
        self.P.op(eng, lambda e: e.tensor_tensor(out=out, in0=in0, in1=in1, op=op), rd, wr, cost=self.vcost(eng, out, 2))

    def TS(self, eng, out, in0, s1, s2, op0, op1, rd, wr):
        if s2 is None:
            self.P.op(eng, lambda e: e.tensor_scalar(out=out, in0=in0, scalar1=s1, scalar2=None, op0=op0), rd, wr, cost=self.vcost(eng, out, 1))
        else:
            self.P.op(eng, lambda e: e.tensor_scalar(out=out, in0=in0, scalar1=s1, scalar2=s2, op0=op0, op1=op1), rd, wr, cost=self.vcost(eng, out, 1))

    def STT(self, eng, out, in0, scalar, in1, op0, op1, rd, wr):
        self.P.op(eng, lambda e: e.scalar_tensor_tensor(out=out, in0=in0, scalar=scalar, in1=in1, op0=op0, op1=op1), rd, wr, cost=self.vcost(eng, out, 2))

    def CPY(self, eng, out, in_, rd, wr):
        if eng == "act":
            self.P.op("act", lambda e: e.activation(out=out, in_=in_, func=AF.Copy), rd, wr, cost=0.2 + self.fsz(out) / 1200.0)
        else:
            self.P.op(eng, lambda e: e.tensor_copy(out=out, in_=in_), rd, wr, cost=self.vcost(eng, out, 1))

    def MSET(self, eng, out, val, wr):
        self.P.op(eng, lambda e: e.memset(out, val), (), wr, cost=self.vcost(eng, out, 1))

    def vcost(self, eng, out, nin):
        f = self.fsz(out)
        if eng == "pool":
            return 0.3 + f * (2.6 if nin == 2 else (3.4 if f >= 1024 else 1.2)) / 1200.0
        return 0.16 + f / 960.0

    def DMA(self, q, out, in_, rd, wr):
        return self.P.dma(q, out, in_, rd, wr)

    def dram(self, name, shape, dt, kind=None):
        if kind is None and self.debug:
            kind = "ExternalOutput"
        if kind is None:
            return self.nc.dram_tensor(name, list(shape), dt).ap()
        return self.nc.dram_tensor(name, list(shape), dt, kind=kind).ap()

    def setup(self, st):
        nc = self.nc
        di = lambda n, s: nc.dram_tensor(n, list(s), F32, kind="ExternalInput").ap()
        self.x = di("x", [S, D])
        self.ccol_d = di("ccol", [128, 8])
        self.w_ada = di("w_ada", [D, 6 * D])
        self.b_ada = di("b_ada", [1, 6 * D])
        self.norm1_w = di("norm1_w", [1, D])
        self.norm2_w = di("norm2_w", [1, D])
        self.norm_f_w = di("norm_f_w", [1, D])
        self.w_in = di("w_in", [D, NIN])
        self.w_att_out = di("w_att_out", [512, D])
        self.w_rwkv_out = di("w_rwkv_out", [D, D])
        self.w_o = di("w_o", [D, D])
        self.w_up = di("w_up", [D, 2 * DFF])
        self.w_down = di("w_down", [DFF, D])
        self.w2a2_d = di("w2a2", [128, D])
        self.g2_d = di("g2", [160, D])
        self.colp_d = di("colp", [128, NCOL])
        self.ident_d = di("ident", [128, 128])
        self.maskA_d = di("maskA", [128, 256])
        self.mask2_d = di("mask2", [128, 192])
        self.bones_d = di("bones", [128, 128])
        self.sel_d = di("sel65", [128, 64])
        self.scanm_d = di("scanm", [128, 256])
        self.out = nc.dram_tensor("out", [S, D], F32, kind="ExternalOutput").ap()
        self.PF = self.dram("PF", [NIN, S], BF16, kind=("ExternalInput" if os.environ.get("KD_ONLY") else None))
        self.VT = self.dram("VT", [3, S, 512], BF16)
        self.YAT = self.dram("YAT", [512, S], BF16)
        self.YRT = self.dram("YRT", [D, S], BF16)
        self.X1 = self.dram("X1", [S, D], F32)
        self.ACTS = self.dram("ACTS", [DFF, S], BF16)
        self.ps = st.enter_context(nc.psum_tensor("ps", [128, 8, 512], F32))
        self.PB = [Region("psb%d" % i) for i in range(8)]
        self.ident_f = self.sb(st, [128, 128], F32, "identf")
        self.ident_b = self.sb(st, [128, 128], BF16, "identb")
        self.colp = self.sb(st, [128, NCOL], F32, "colp")
        self.GT1 = self.sb(st, [128, D], F32, "gt1")
        self.GT2 = self.sb(st, [128, D], F32, "gt2")
        self.A1c = self.sb(st, [128, 8], F32, "a1c")
        self.S1c = self.sb(st, [128, 8], F32, "s1c")
        self.A2c = self.sb(st, [128, 8], F32, "a2c")
        self.S2c = self.sb(st, [128, 8], F32, "s2c")
        self.epsc = self.sb(st, [128, 2], F32, "epsc")
        self.DMA("sp", self.ident_f[:], self.ident_d, (), [self.ident_f.r])
        self.DMA("sp", self.colp[:], self.colp_d, (), [self.colp.r])
        self.CPY("dve", self.ident_b[:], self.ident_f[:], [self.ident_f.r], [self.ident_b.r])
        self.MSET("pool", self.epsc[:, 0:1], RMS_EPS, [self.epsc.r])
        self.MSET("pool", self.epsc[:, 1:2], GN_EPS, [self.epsc.r])

    def col(self, name, j=0, rows=slice(0, 128)):
        return self.colp[rows, CP[name] + j:CP[name] + j + 1]

    def phase0(self):
        ps, PB = self.ps, self.PB
        with ExitStack() as st:
            ccol = self.sb(st, [128, 8], F32, "ccol")
            bada = self.sb(st, [128, 6 * D], F32, "bada")
            ADA = self.sb(st, [128, 6 * D], F32, "ada")
            wa = [self.sb(st, [128, 3072], F32, "wa") for _ in range(3)]
            nw = [self.sb(st, [128, D], F32, "nw") for _ in range(2)]
            tmp = self.sb(st, [128, D], F32, "tmp")
            tmp2 = self.sb(st, [128, D], F32, "tmp2")
            self.DMA("sp", ccol[:], self.ccol_d, (), [ccol.r])
            self.DMA("pool", bada[:], self.b_ada.partition_broadcast(128), (), [bada.r])
            self.DMA("pool", nw[0][:], self.norm1_w.partition_broadcast(128), (), [nw[0].r])
            self.DMA("pool", nw[1][:], self.norm2_w.partition_broadcast(128), (), [nw[1].r])
            i = 0
            for half in range(2):
                for kc in range(8):
                    buf = wa[i % 3]
                    i += 1
                    self.DMA("sp", buf[:], self.w_ada[kc * 128:(kc + 1) * 128, half * 3072:(half + 1) * 3072], (), [buf.r])
                    for j in range(6):
                        self.MM(ps[:, j, :], ccol[:, kc:kc + 1].to_broadcast([128, 128]), buf[:, j * 512:(j + 1) * 512],
                                kc == 0, kc == 7, [ccol.r, buf.r], [PB[j]])
                for j in range(6):
                    o = half * 3072 + j * 512
                    self.TT("dve", ADA[:, o:o + 512], ps[:, j, :], bada[:, o:o + 512], ALU.add, [PB[j], bada.r], [ADA.r])
            idb = self.ident_f[:].unsqueeze(1).to_broadcast([128, 8, 128])
            v3 = lambda t: t[:].rearrange("p (k f) -> p k f", f=128)

            def diag(dst, src_ap_region, src):
                self.TT("dve", v3(tmp2), src, idb, ALU.mult, [src_ap_region, self.ident_f.r], [tmp2.r])
                self.P.op("dve", lambda e: e.tensor_reduce(out=dst[:], in_=v3(tmp2), axis=AX.X, op=ALU.add), [tmp2.r], [dst.r])
            for (sc_o, sh_o, nwt, Ac, Sc) in ((1024, 0, nw[0], self.A1c, self.S1c), (4096, 3072, nw[1], self.A2c, self.S2c)):
                self.STT("dve", tmp[:], ADA[:, sc_o:sc_o + D], 1.0, nwt[:], ALU.add, ALU.mult, [ADA.r, nwt.r], [tmp.r])
                diag(Ac, tmp.r, v3(tmp))
                diag(Sc, ADA.r, ADA[:, sh_o:sh_o + D].rearrange("p (k f) -> p k f", f=128))
            self.CPY("dve", self.GT1[:], ADA[:, 2048:3072], [ADA.r], [self.GT1.r])
            self.CPY("dve", self.GT2[:], ADA[:, 5120:6144], [ADA.r], [self.GT2.r])
            self.P.barrier()
            self.P.flush()

    def norm_block(self, xt, xs, junk, ssq, rs):
        self.MSET("pool", ssq[:], 0.0, [ssq.r])
        self.ACT(junk[:], xt[:], AF.Square, [xt.r, ssq.r], [junk.r, ssq.r], accum=ssq[:])
        self.ACT(rs[:], ssq[:], AF.Sqrt, [ssq.r, self.epsc.r], [rs.r], bias=self.epsc[:, 0:1], scale=1.0 / D)
        self.P.op("dve", lambda e: e.reciprocal(out=rs[:], in_=rs[:]), [rs.r], [rs.r], cost=0.2)
        self.ACT(xs[:], xt[:], AF.Identity, [xt.r, rs.r], [xs.r], scale=rs[:, 0:1])

    def transpose_group(self, xs2, g, Ac, Sc, hT, W=256):
        ps, PB = self.ps, self.PB
        banks = [self.bank() for _ in range(4)]
        for blk in range(2):
            for kc in range(8):
                b = banks[kc // 2]
                o = (kc % 2) * 256 + blk * 128
                self.TR(ps[:, b, o:o + 128], xs2[blk][:, kc * 128:(kc + 1) * 128], self.ident_f[:],
                        [xs2[blk].r, self.ident_f.r], [PB[b]])
        for kc in range(8):
            b = banks[kc // 2]
            o = (kc % 2) * 256
            self.ACT(hT[:, kc, g * 256:(g + 1) * 256], ps[:, b, o:o + 256], AF.Identity, [PB[b], Ac.r, Sc.r], [hT.r],
                     bias=Sc[:, kc:kc + 1], scale=Ac[:, kc:kc + 1])

    def phaseA(self, hT):
        with ExitStack() as st:
            xt = [self.sb(st, [128, D], F32, "xt") for _ in range(3)]
            xs = [self.sb(st, [128, D], F32, "xs") for _ in range(4)]
            junk = self.sb(st, [128, D], F32, "junk")
            ssq = [self.sb(st, [128, 1], F32, "ssq") for _ in range(2)]
            rs = [self.sb(st, [128, 1], F32, "rs") for _ in range(2)]
            if os.environ.get('KD_SCHED', '1') == '1':
                self.P.defer = True
            for g in range(S // 256):
                pair = []
                for blk in range(2):
                    i = g * 2 + blk
                    self.DMA("sp", xt[i % 3][:], self.x[i * 128:(i + 1) * 128, :], (), [xt[i % 3].r])
                    self.norm_block(xt[i % 3], xs[i % 4], junk, ssq[i % 2], rs[i % 2])
                    pair.append(xs[i % 4])
                self.transpose_group(pair, g, self.A1c, self.S1c, hT)
            if self.P.defer:
                self.P.schedule()
            self.P.barrier()
            self.P.flush()

    def phaseB(self, hT):
        ps, PB = self.ps, self.PB
        blocks = []
        for g in range(3):
            blocks.append((g * 1536, 512, "f"))
            blocks.append((g * 1536 + 512, 512, "f"))
            blocks.append((g * 1536 + 1024, 512, ("v", g)))
        for c0 in range(RW0, 7680, 512):
            blocks.append((c0, 512, "f"))
        blocks.append((7680, 288, "f"))
        for c0 in range(GATE0, NIN, 512):
            blocks.append((c0, 512, "f"))
        with ExitStack() as st:
            wf = [self.sb(st, [128, 8, 512], F32, "wf") for _ in range(2)]
            wb = [self.sb(st, [128, 8, 512], BF16, "wb") for _ in range(2)]
            stg = [self.sb(st, [128, 4, 512], BF16, "stg") for _ in range(4)]
            si = 0
            for bi, (c0, cw, kind) in enumerate(blocks):
                f_, b_ = wf[bi % 2], wb[bi % 2]
                self.DMA("pool", f_[:, :, 0:cw], self.w_in[:, c0:c0 + cw].rearrange("(k p) n -> p k n", p=128), (), [f_.r])
                self.CPY("pool", b_[:, :, 0:cw], f_[:, :, 0:cw], [f_.r], [b_.r])
                for tt in range(8):
                    sg = stg[si % 4]
                    si += 1
                    t0 = tt * 512
                    if kind == "f":
                        nch = (cw + 127) // 128
                        for oc in range(nch):
                            w = min(128, cw - oc * 128)
                            b = self.bank()
                            for kc in range(8):
                                self.MM(ps[0:w, b, :], b_[:, kc, oc * 128:oc * 128 + w], hT[:, kc, t0:t0 + 512],
                                        kc == 0, kc == 7, [b_.r, hT.r], [PB[b]])
                            self.CPY(self.ev(), sg[0:w, oc, :], ps[0:w, b, :], [PB[b]], [sg.r])
                        if cw == 512:
                            self.DMA("sp", self.PF[c0:c0 + 512, t0:t0 + 512].rearrange("(o p) t -> p o t", p=128), sg[:], [sg.r], [Region()])
                        else:
                            for oc in range(nch):
                                w = min(128, cw - oc * 128)
                                self.DMA("sp", self.PF[c0 + oc * 128:c0 + oc * 128 + w, t0:t0 + 512], sg[0:w, oc, :], [sg.r], [Region()])
                    else:
                        g = kind[1]
                        for sbk in range(4):
                            b = self.bank()
                            for kc in range(8):
                                self.MM(ps[:, b, :], hT[:, kc, t0 + sbk * 128:t0 + (sbk + 1) * 128], b_[:, kc, :],
                                        kc == 0, kc == 7, [b_.r, hT.r], [PB[b]])
                            self.CPY(self.ev(), sg[:, sbk, :], ps[:, b, :], [PB[b]], [sg.r])
                        self.DMA("sp", self.VT[g, t0:t0 + 512, :].rearrange("(s p) f -> p s f", p=128), sg[:], [sg.r], [Region()])
            self.P.barrier()
            self.P.flush()


def _colv(v):
    v = np.asarray(v, np.float32).reshape(-1)
    n = (v.size + 127) // 128
    buf = np.zeros(n * 128, np.float32)
    buf[:v.size] = v
    return np.ascontiguousarray(buf.reshape(n, 128).T)


def build_nc(stop_after=None, debug=False):
    nc = bass.Bass("TRN2", target_bir_lowering=False)
    kb = KB(nc, debug=debug)
    build_nc.kb = kb
    with ExitStack() as st:
        kb.P.alloc_sems(st)
        kb.setup(st)
        kb.run(st, stop_after)
    return nc


def make_inputs(inp):
    f = lambda a: np.ascontiguousarray(np.asarray(a, np.float32))
    mu = f(inp["mu_shift"][0])
    colp = np.zeros((128, NCOL), np.float32)

    def put(name, arr):
        colp[:, CP[name]:CP[name] + arr.shape[1]] = arr
    put("bgate", _colv(inp["b_gate"][0]))
    put("mu_rkv", _colv(mu[0:3072]))
    put("mu_wa", _colv(mu[3072:3200]))
    put("mu_g", _colv(mu[3200:3360]))
    put("w0", _colv(inp["w0"][0])); put("a0", _colv(inp["a0"][0])); put("kk", _colv(inp["k_k"][0]))
    put("ka", _colv(inp["k_a"][0])); put("rk", _colv(inp["r_k"][0])); put("lnw", _colv(inp["lnx_w"][0]))
    put("lnb", _colv(inp["lnx_b"][0]))
    cw = f(inp["conv_w"][0])
    put("cw0", _colv(cw[0])); put("cw1", _colv(cw[1])); put("cw2", _colv(cw[2])); put("cb", _colv(inp["conv_b"][0]))
    k = np.arange(128)[:, None]
    q = np.arange(128)[None, :]
    NEG = -30000.0
    maskA = np.concatenate([np.where(k <= q, 0.0, NEG), np.where(k >= q, 0.0, NEG)], axis=1).astype(np.float32)
    j = (np.arange(128) % 64)[:, None]
    i = np.arange(64)[None, :]
    mask2 = np.concatenate([(i > j), (i >= j), (i < j)], axis=1).astype(np.float32)
    bones = np.zeros((128, 128), np.float32)
    bones[0:64, 0:64] = 1.0
    bones[64:128, 64:128] = 1.0
    sel = np.zeros((128, 64), np.float32)
    sel[64, :] = 1.0
    scanm = np.ones((128, 256), np.float32)
    scanm[:, 0::64] = 0.0
    shared = {
        "w_ada": f(inp["w_ada"][0]), "b_ada": f(inp["b_ada"][0]).reshape(1, -1),
        "norm1_w": f(inp["norm1_w"][0]).reshape(1, -1), "norm2_w": f(inp["norm2_w"][0]).reshape(1, -1),
        "norm_f_w": f(inp["norm_f_w"]).reshape(1, -1), "w_in": f(inp["w_in"][0]),
        "w_att_out": f(inp["w_att_out"][0]), "w_rwkv_out": f(inp["w_rwkv_out"][0]), "w_o": f(inp["w_o"][0]),
        "w_up": f(inp["w_up"][0]), "w_down": f(inp["w_down"][0]),
        "w2a2": np.ascontiguousarray(np.concatenate([f(inp["w2"][0]), f(inp["a2"][0])], axis=0)),
        "g2": f(inp["g2"][0]), "colp": colp, "ident": np.eye(128, dtype=np.float32), "maskA": maskA,
        "mask2": mask2, "bones": bones, "sel65": sel, "scanm": scanm,
    }
    maps = []
    for b in range(8):
        m = dict(shared)
        m["x"] = f(inp["x"][b])
        m["ccol"] = _colv(inp["c"][b])
        maps.append(m)
    return maps


def kernel(**inputs):
    nc = build_nc()
    maps = make_inputs(inputs)
    res = run_bass_kernel_spmd(nc, maps, core_ids=list(range(8)))
    return np.stack([np.asarray(r["out"], np.float32) for r in res.results], axis=0)


def _run(self, st, stop_after=None):
    if os.environ.get("KD_ONLY") == "D":
        self.P.barrier()
        self.P.flush()
        self.phaseD()
        return
    self.phase0()
    with ExitStack() as s1:
        hT = self.sb(s1, [128, 8, S], BF16, "hT")
        self.phaseA(hT)
        if stop_after == "A":
            return
        self.phaseB(hT)
    if stop_after == "B":
        return
    if stop_after != "D":
        self.phaseC()
    if stop_after == "C":
        return
    self.phaseD()
    if stop_after == "D":
        return
    with ExitStack() as s2:
        hT2 = self.sb(s2, [128, 8, S], BF16, "hT2")
        self.phaseE(hT2)
        if stop_after == "E":
            return
        self.phaseF1(hT2)
    if stop_after == "F1":
        return
    self.phaseF2()


KB.run = _run


def _phaseC(self):
    ps, PB = self.ps, self.PB
    with ExitStack() as st:
        maskf = self.sb(st, [128, 256], F32, "maskf")
        maskb = self.sb(st, [128, 256], BF16, "maskb")
        sel = self.sb(st, [128, 64], F32, "sel")
        qk = [self.sb(st, [128, 2, S], BF16, "qk") for _ in range(2)]
        Vs = [self.sb(st, [128, 32, 2, 65], BF16, "Vs") for _ in range(2)]
        ACC = self.sb(st, [128, 2, S], F32, "ACC")
        PT = self.sb(st, [128, 4, 16, 256], BF16, "PT")
        PTr = [[Region("pt") for _ in range(16)] for _ in range(4)]
        rec = [self.sb(st, [64, 512], F32, "rec") for _ in range(2)]
        ya = [self.sb(st, [64, 512], BF16, "ya") for _ in range(2)]
        self.DMA("sp", maskf[:], self.maskA_d, (), [maskf.r])
        self.DMA("sp", sel[:], self.sel_d, (), [sel.r])
        self.CPY("dve", maskb[:], maskf[:], [maskf.r], [maskb.r])
        for v in Vs:
            self.MSET("pool", v[:], 1.0, [v.r])
        if os.environ.get('KD_SCHED', '1') == '1':
            self.P.defer = True
        sb_i = [0]
        WARM = os.environ.get('KD_WARM', '0') == '1'
        dz = self.sb(st, [128, 512], BF16, "dz")
        self.MSET("pool", dz[:], 0.0, [dz.r])
        PBd = Region("psdummy")
        pb_i = [0]

        def sbank():
            b = sb_i[0]
            sb_i[0] = (b + 1) % (5 if WARM else 6)
            return b

        it = 0
        fin_i = 0
        for hp in range(4):
            for g in range(3):
                d = ATT_D[g]
                nb = S // (128 * d)
                buf, V = qk[it % 2], Vs[it % 2]
                it += 1
                c0 = g * 1536 + hp * 128
                self.DMA("sp", buf[:, 0, :], self.PF[c0:c0 + 128, :], (), [buf.r])
                self.DMA("sp", buf[:, 1, :], self.PF[c0 + 512:c0 + 640, :], (), [buf.r])
                vsrc = self.VT[g].rearrange("(n j r) f -> r j n f", j=128, r=d)
                for r in range(d):
                    for h2 in range(2):
                        f0 = hp * 128 + h2 * 64
                        self.DMA("pool", V[:, r * nb:(r + 1) * nb, h2, 0:64], vsrc[r, :, :, f0:f0 + 64], (), [V.r])
                qv = buf[:, 0, :].rearrange("p (n i r) -> p r n i", i=128, r=d)
                kv = buf[:, 1, :].rearrange("p (n i r) -> p r n i", i=128, r=d)
                for h2 in range(2):
                    hs = slice(h2 * 64, (h2 + 1) * 64)
                    state = {"pob": None}

                    def emit_S(kb, r):
                        nq = 2 if kb + 1 < nb else 1
                        b = sbank()
                        self.MM(ps[:, b, 0:nq * 128].rearrange("p (a i) -> p a i", i=128), kv[hs, r, kb, :],
                                qv[hs, r, kb:kb + nq, :], True, False, [buf.r], [PB[b]])
                        self.MM(ps[:, b, 0:nq * 128], self.ident_b[:], maskb[:, 0:nq * 128], False, True,
                                [self.ident_b.r, maskb.r], [PB[b]])
                        self.ACT(PT[:, kb % 4, r, 0:nq * 128], ps[:, b, 0:nq * 128], AF.Exp, [PB[b]], [PTr[kb % 4][r]], scale=0.125)
                        if WARM:
                            self.MM(ps[:, 5, :], self.ident_b[:], dz[:], True, True, [self.ident_b.r, dz.r, PB[b]], [PBd])

                    def emit_O(kb, r):
                        slot = (kb % 4) if d == 1 else (r % 4)
                        if slot == 0:
                            state["pob"] = 6 + pb_i[0]
                            pb_i[0] ^= 1
                        pob = state["pob"]
                        po = ps[0:65, pob, slot * 128:(slot + 1) * 128]
                        if kb > 0:
                            self.MM(po, V[:, r * nb + kb - 1, h2, :], PT[:, (kb - 1) % 4, r, 128:256], True, False,
                                    [V.r, PTr[(kb - 1) % 4][r]], [PB[pob]])
                        self.MM(po, V[:, r * nb + kb, h2, :], PT[:, kb % 4, r, 0:128], kb == 0, True,
                                [V.r, PTr[kb % 4][r]], [PB[pob]])
                        if slot == 3:
                            if d == 1:
                                t0 = (kb // 4) * 512
                                av = ACC[0:65, h2, t0:t0 + 512]
                                pv = ps[0:65, pob, :]
                            else:
                                av = ACC[0:65, h2, kb * 128 * d:(kb + 1) * 128 * d].rearrange("p (i r) -> p r i", r=d)[:, r - 3:r + 1, :]
                                pv = ps[0:65, pob, :].rearrange("p (r i) -> p r i", i=128)
                            if g == 0:
                                self.CPY("dve", av, pv, [PB[pob]], [ACC.r])
                            else:
                                self.TT("dve", av, pv, av, ALU.add, [PB[pob], ACC.r], [ACC.r])

                    steps = [(kb, r) for kb in range(nb) for r in range(d)]
                    LAG = 2
                    for i, (kb, r) in enumerate(steps):
                        emit_S(kb, r)
                        if i >= LAG:
                            emit_O(*steps[i - LAG])
                    for j in range(max(0, len(steps) - LAG), len(steps)):
                        emit_O(*steps[j])
            for h2 in range(2):
                for tt in range(8):
                    b = sbank()
                    rc, yy = rec[fin_i % 2], ya[fin_i % 2]
                    fin_i += 1
                    ts_ = slice(tt * 512, (tt + 1) * 512)
                    self.MM(ps[0:64, b, :], sel[0:65, :], ACC[0:65, h2, ts_], True, True, [sel.r, ACC.r], [PB[b]])
                    self.ACT(rc[:], ps[0:64, b, :], AF.Ln, [PB[b]], [rc.r])
                    self.ACT(rc[:], rc[:], AF.Exp, [rc.r], [rc.r], scale=-1.0)
                    self.TT("pool", yy[:], ACC[0:64, h2, ts_], rc[:], ALU.mult, [ACC.r, rc.r], [yy.r])
                    r0 = hp * 128 + h2 * 64
                    self.DMA("sp", self.YAT[r0:r0 + 64, ts_], yy[:], [yy.r], [Region()])
        if self.P.defer:
            self.P.schedule()
        self.P.barrier()
        self.P.flush()


KB.phaseC = _phaseC


C0 = 0.6065306597126334
TW = 256


def _phaseD(self):
    ps, PB = self.ps, self.PB
    P = self.P
    with ExitStack() as st:
        sb = lambda shape, dt, name: self.sb(st, shape, dt, name)
        W2A2 = sb([128, D], BF16, "w2a2")
        G2a = sb([128, D], BF16, "g2a")
        G2b = sb([32, D], BF16, "g2b")
        with ExitStack() as st2:
            wst = self.sb(st2, [128, D], F32, "wst")
            self.DMA("sp", wst[:], self.w2a2_d, (), [wst.r])
            self.CPY("dve", W2A2[:], wst[:], [wst.r], [W2A2.r])
            self.DMA("sp", wst[:], self.g2_d[0:128, :], [], [wst.r])
            self.CPY("dve", G2a[:], wst[:], [wst.r], [G2a.r])
            self.DMA("sp", wst[0:32, :], self.g2_d[128:160, :], [], [wst.r])
            self.CPY("dve", G2b[:], wst[0:32, :], [wst.r], [G2b.r])
            self.P.barrier()
            self.P.flush()
        mask2 = sb([128, 192], F32, "mask2")
        bones = sb([128, 128], F32, "bones")
        scanm = sb([128, TW], F32, "scanm")
        self.DMA("sp", mask2[:], self.mask2_d, (), [mask2.r])
        self.DMA("sp", bones[:], self.bones_d, (), [bones.r])
        self.DMA("sp", scanm[:], self.scanm_d, (), [scanm.r])
        idn64 = sb([128, 64], BF16, "idn64")
        self.TT("dve", idn64[:], self.ident_f[:, 0:64], self.ident_f[:, 64:128], ALU.add, [self.ident_f.r], [idn64.r])
        omk = sb([128, 8], F32, "omk")
        self.TS("dve", omk[:], self.colp[:, CP["ka"]:CP["ka"] + 8], -1.0, 1.0, ALU.mult, ALU.add, [self.colp.r], [omk.r])
        tiny = sb([128, 1], F32, "tiny")
        self.MSET("pool", tiny[:], 1e-24, [tiny.r])
        H32 = sb([128, 8, 64], F32, "H32")
        Hbfs = [sb([128, 8, 64], BF16, "Hbf") for _ in range(2)]
        self.MSET("pool", H32[:], 0.0, [H32.r])
        self.MSET("pool", Hbfs[0][:], 0.0, [Hbfs[0].r])
        ZB = [sb([128, 6, TW + 1], BF16, "ZB") for _ in range(2)]
        ZL = [sb([128, 3, TW + 1], BF16, "ZL") for _ in range(2)]
        tset = [{n: sb([128, 2, TW], F32, n) for n in "R K V D1 D2 D3 SW A kk0 ta".split()} for _ in range(1)]
        tsq = [sb([128, 2, TW], BF16, "tsq") for _ in range(2)]
        trk = [sb([128, 2, TW], BF16, "trk") for _ in range(1)]
        dl = sb([128, TW], F32, "dl")
        bones_b = sb([128, 128], BF16, "bonesb")
        self.CPY("dve", bones_b[:], bones[:], [bones.r], [bones_b.r])
        omu = sb([128, 24], F32, "omu")
        self.TS("dve", omu[:], self.colp[:, CP["mu_rkv"]:CP["mu_rkv"] + 24], -1.0, 1.0, ALU.mult, ALU.add, [self.colp.r], [omu.r])
        WAl = sb([128, TW], F32, "WAl"); G0 = sb([128, TW], F32, "G0"); G1 = sb([32, TW], F32, "G1")
        TWA = sb([128, TW], BF16, "TWA"); SG0 = sb([128, TW], BF16, "SG0"); SG1 = sb([32, TW], BF16, "SG1")
        Gts = [sb([128, 8, TW], BF16, "Gt") for _ in range(2)]
        bonuss = [sb([128, 8, TW], BF16, "bonus") for _ in range(2)]
        PCs = [sb([128, 8, 4], F32, "PC") for _ in range(2)]
        ARs = [sb([128, 8, 4, 2, 64], BF16, "AR") for _ in range(2)]
        Afs = [sb([128, 8, TW], BF16, "Af") for _ in range(2)]; Bfs = [sb([128, 8, TW], BF16, "Bf") for _ in range(2)]
        Kfs = [sb([128, 8, TW], BF16, "Kf") for _ in range(2)]; Vbs = [sb([128, 8, TW], BF16, "Vb") for _ in range(2)]
        Atm = sb([128, 2, D], BF16, "Atm"); Btm = sb([128, 2, D], BF16, "Btm")
        Ktm = sb([128, 2, D], BF16, "Ktm"); Vtm = sb([128, 2, D], BF16, "Vtm")
        BAs = sb([128, 2, 16, 128], BF16, "BAs"); KAs = sb([128, 2, 16, 128], BF16, "KAs")
        NA = [sb([128, 1, 16, 64], BF16, "NA") for _ in range(2)]
        NB = [sb([128, 1, 16, 64], BF16, "NB") for _ in range(2)]
        NT = [sb([128, 1, 16, 64], BF16, "NT") for _ in range(2)]
        M1T = sb([128, 8, 4, 64], BF16, "M1T")
        AKVs = sb([128, 2, 16, 64], BF16, "AKVs")
        U2s = sb([128, 2, 16, 64], BF16, "U2s")
        KVs = sb([128, 8, 4, 64], BF16, "KVs")
        Utm = sb([128, 2, 16, 64], BF16, "Utm")
        class _V:
            def __init__(self, ap, r):
                self.ap, self.r = ap, r

            def __getitem__(self, k):
                return self.ap[k]
        HKf = sb([128, 512], F32, "HK"); T1f = sb([128, 512], F32, "T1")
        HK = _V(HKf[:].rearrange("p (h x) -> p h x", x=64), HKf.r); T1 = _V(T1f[:].rearrange("p (h x) -> p h x", x=64), T1f.r)
        Hy = _V(HKf[:].rearrange("p (q t) -> p q t", q=2), HKf.r); T1y = _V(T1f[:].rearrange("p (q t) -> p q t", q=2), T1f.r)
        Yf = sb([128, 8, TW], F32, "Yf")
        YOb = [sb([128, 2, TW], BF16, "YO") for _ in range(2)]
        ynb = sb([128, 2, TW], F32, "ynb")

        b2_i = [0]

        def bank2():
            b = b2_i[0]
            b2_i[0] = (b + 2) % 8
            return b

        def lerp(dst, zsrc, ci, mucol, rows=slice(0, 128)):
            prev, cur = zsrc[rows, ci, 0:TW], zsrc[rows, ci, 1:TW + 1]
            dd = dl
            self.TT("pool", dd[rows, :], prev, cur, ALU.subtract, [zsrc.r], [dd.r])
            self.STT("dve", dst[rows, :], dd[rows, :], mucol, cur, ALU.mult, ALU.add, [dd.r, zsrc.r, self.colp.r], [dst.r])

        v4 = lambda ap: ap.rearrange("p (n i) -> p n i", i=64)
        ntile = int(os.environ.get('KD_NT', S // TW))
        lim = int(os.environ.get('KD_STAGE', 99))
        def colb(name, c0):
            o = CP[name] + c0
            return self.colp[:, o:o + 2].unsqueeze(2).to_broadcast([128, 2, TW])

        def load_lora(ti):
            Zl = ZL[ti % 2]
            t0 = ti * TW
            for (c_lo, c_hi, rows) in ((0, 2, 128), (2, 3, 32)):
                r0 = RW0 + (24 + c_lo) * 128
                nr = (c_hi - c_lo - 1) * 128 + rows
                if ti == 0:
                    self.MSET("pool", Zl[0:rows, c_lo:c_hi, 0:1], 0.0, [Zl.r])
                    src, dst = self.PF[r0:r0 + nr, 0:TW], Zl[0:rows, c_lo:c_hi, 1:TW + 1]
                else:
                    src, dst = self.PF[r0:r0 + nr, t0 - 1:t0 + TW], Zl[0:rows, c_lo:c_hi, :]
                self.DMA("sp", dst, src.rearrange("(c p) t -> p c t", p=rows), (), [Zl.r])

        def load_batch(ti, bq):
            Zb = ZB[(ti * 4 + bq) % 2]
            t0 = ti * TW
            for j, base in enumerate((0, 8, 16)):
                r0 = RW0 + (base + 2 * bq) * 128
                if ti == 0:
                    self.MSET("pool", Zb[:, 2 * j:2 * j + 2, 0:1], 0.0, [Zb.r])
                    src, dst = self.PF[r0:r0 + 256, 0:TW], Zb[:, 2 * j:2 * j + 2, 1:TW + 1]
                else:
                    src, dst = self.PF[r0:r0 + 256, t0 - 1:t0 + TW], Zb[:, 2 * j:2 * j + 2, :]
                self.DMA("sp", dst, src.rearrange("(c p) t -> p c t", p=128), (), [Zb.r])

        def P_gen(ti):
            t0 = ti * TW
            Zl = ZL[ti % 2]
            AR, Bf, Kf, Af, Vb, Gt, bonus, PC = (X_[ti % 2] for X_ in (ARs, Bfs, Kfs, Afs, Vbs, Gts, bonuss, PCs))
            load_lora(ti)
            load_batch(ti, 0)
            yield
            lerp(WAl, Zl, 0, self.col("mu_wa"))
            lerp(G0, Zl, 1, self.col("mu_g", 0))
            lerp(G1, Zl, 2, self.col("mu_g", 1, slice(0, 32)), rows=slice(0, 32))
            self.ACT(TWA[0:64, :], WAl[0:64, :], AF.Tanh, [WAl.r], [TWA.r])
            self.CPY("dve", TWA[64:128, :], WAl[64:128, :], [WAl.r], [TWA.r])
            self.ACT(SG0[:], G0[:], AF.Sigmoid, [G0.r], [SG0.r])
            self.ACT(SG1[:], G1[:], AF.Sigmoid, [G1.r], [SG1.r])
            yield
            for bq in range(4):
                gi = ti * 4 + bq
                Zb = ZB[gi % 2]
                if bq + 1 < 4:
                    load_batch(ti, bq + 1)
                Tt = tset[0]
                R_, K_, V_, D1, D2, D3, SW, A_, kk0, ta = (Tt[n] for n in "R K V D1 D2 D3 SW A kk0 ta".split())
                sq, rkm = tsq[gi % 2], trk[0]
                c0 = 2 * bq
                for j, (X_, Dj) in enumerate(((R_, D1), (K_, D2), (V_, D3))):
                    for q in range(2):
                        mi = 8 * j + c0 + q
                        self.ACT(Dj[:, q, :], Zb[:, 2 * j + q, 0:TW], AF.Identity, [Zb.r, self.colp.r], [Dj.r], scale=self.col("mu_rkv", mi))
                        self.STT("dve", X_[:, q, :], Zb[:, 2 * j + q, 1:TW + 1], omu[:, mi:mi + 1], Dj[:, q, :], ALU.mult, ALU.add,
                                 [Zb.r, omu.r, Dj.r], [X_.r])
                bw, ba, bg = self.bank(), self.bank(), self.bank()
                for q in range(2):
                    cs = slice((c0 + q) * 128, (c0 + q + 1) * 128)
                    qs = slice(q * TW, (q + 1) * TW)
                    self.MM(ps[:, bw, qs], W2A2[0:64, cs], TWA[0:64, :], True, True, [W2A2.r, TWA.r], [PB[bw]])
                    self.MM(ps[:, ba, qs], W2A2[64:128, cs], TWA[64:128, :], True, True, [W2A2.r, TWA.r], [PB[ba]])
                for q in range(2):
                    cs = slice((c0 + q) * 128, (c0 + q + 1) * 128)
                    qs = slice(q * TW, (q + 1) * TW)
                    self.MM(ps[:, bg, qs], G2a[:, cs], SG0[:], True, False, [G2a.r, SG0.r], [PB[bg]])
                    self.MM(ps[:, bg, qs], G2b[0:32, cs], SG1[0:32, :], False, True, [G2b.r, SG1.r], [PB[bg]])
                for q in range(2):
                    qs = slice(q * TW, (q + 1) * TW)
                    self.ACT(SW[:, q, :], ps[:, bw, qs], AF.Sigmoid, [PB[bw], self.colp.r], [SW.r], bias=self.col("w0", c0 + q))
                    self.ACT(A_[:, q, :], ps[:, ba, qs], AF.Sigmoid, [PB[ba], self.colp.r], [A_.r], bias=self.col("a0", c0 + q))
                p3 = lambda b: ps[:, b, :].rearrange("p (q t) -> p q t", q=2)
                self.CPY("act", Gt[:, c0:c0 + 2, :], p3(bg), [PB[bg]], [Gt.r])
                for q in range(2):
                    P.op("dve", lambda e, o=D1[:, q, :], m=scanm[:], s_=SW[:, q, :]: e.tensor_tensor_scan(out=o, data0=m, data1=s_, initial=0.0, op0=ALU.mult, op1=ALU.add),
                         [scanm.r, SW.r], [D1.r])
                self.TT("pool", SW[:], D1[:], SW[:], ALU.subtract, [D1.r, SW.r], [SW.r])
                self.ACT(D2[:], D1[:], AF.Exp, [D1.r], [D2.r], scale=-C0)
                self.ACT(D1[:], D1[:], AF.Exp, [D1.r], [D1.r], scale=C0)
                self.ACT(SW[:], SW[:], AF.Exp, [SW.r], [SW.r], scale=-C0)
                self.CPY("pool", PC[:, c0:c0 + 2, :], D2[:].rearrange("p q (n i) -> p q n i", i=64)[:, :, :, 63], [D2.r], [PC.r])
                yield
                self.TT("dve", kk0[:], K_[:], colb("kk", c0), ALU.mult, [K_.r, self.colp.r], [kk0.r])
                self.ACT(sq[:], kk0[:], AF.Square, [kk0.r], [sq.r])
                bs = self.bank()
                for q in range(2):
                    self.MM(ps[:, bs, q * TW:(q + 1) * TW], bones_b[:], sq[:, q, :], True, True, [bones_b.r, sq.r], [PB[bs]])
                self.ACT(D3[:], p3(bs), AF.Ln, [PB[bs], tiny.r], [D3.r], bias=tiny[:, 0:1])
                self.ACT(D3[:], D3[:], AF.Exp, [D3.r], [D3.r], scale=-0.5)
                self.TT("dve", kk0[:], kk0[:], D3[:], ALU.mult, [kk0.r, D3.r], [kk0.r])
                for q in range(2):
                    self.TS("dve", ta[:, q, :], A_[:, q, :], self.col("ka", c0 + q), omk[:, c0 + q:c0 + q + 1], ALU.mult, ALU.add,
                            [A_.r, self.colp.r, omk.r], [ta.r])
                self.TT("pool", ta[:], K_[:], ta[:], ALU.mult, [K_.r, ta.r], [ta.r])
                self.TT("pool", A_[:], kk0[:], A_[:], ALU.mult, [kk0.r, A_.r], [A_.r])
                v5 = lambda ap: ap.rearrange("p q (n i) -> p q n i", i=64)
                self.TT("dve", AR[:, c0:c0 + 2, :, 1, :], v5(R_[:]), v5(D2[:]), ALU.mult, [R_.r, D2.r], [AR.r])
                for q in range(2):
                    self.STT("dve", rkm[:, q, :], R_[:, q, :], self.col("rk", c0 + q), ta[:, q, :], ALU.mult, ALU.mult,
                             [R_.r, ta.r, self.colp.r], [rkm.r])
                bb_ = self.bank()
                for q in range(2):
                    self.MM(ps[:, bb_, q * TW:(q + 1) * TW], bones_b[:], rkm[:, q, :], True, True, [bones_b.r, rkm.r], [PB[bb_]])
                self.TT("dve", bonus[:, c0:c0 + 2, :], p3(bb_), V_[:], ALU.mult, [PB[bb_], V_.r], [bonus.r])
                self.STT("dve", Af[:, c0:c0 + 2, :], kk0[:], -1.0, SW[:], ALU.mult, ALU.mult, [kk0.r, SW.r], [Af.r])
                self.CPY("pool", AR[:, c0:c0 + 2, :, 0, :], v5(Af[:, c0:c0 + 2, :]), [Af.r], [AR.r])
                self.TT("pool", Bf[:, c0:c0 + 2, :], A_[:], D1[:], ALU.mult, [A_.r, D1.r], [Bf.r])
                self.TT("dve", Kf[:, c0:c0 + 2, :], ta[:], D1[:], ALU.mult, [ta.r, D1.r], [Kf.r])
                self.CPY("act", Vb[:, c0:c0 + 2, :], V_[:], [V_.r], [Vb.r])
                yield
        def Q_gen(ti):
            t0 = ti * TW
            AR, Bf, Kf, Af, Vb, Gt, bonus, PC = (X_[ti % 2] for X_ in (ARs, Bfs, Kfs, Afs, Vbs, Gts, bonuss, PCs))
            for (src, dst) in ((Af, Atm), (Bf, Btm), (Kf, Ktm), (Vb, Vtm)):
                for blk in range(2):
                    b = self.bank()
                    psb = ps[:, b, :].bitcast(BF16)
                    for c in range(8):
                        self.TR(psb[:, c * 128:(c + 1) * 128], src[:, c, blk * 128:(blk + 1) * 128], self.ident_b[:],
                                [src.r, self.ident_b.r], [PB[b]])
                    self.CPY(self.ev(), dst[:, blk, :], psb, [PB[b]], [dst.r])
            yield
            for units in ([(0, 0), (0, 1)], [(1, 0), (1, 1)]):
                for (blk, half) in units:
                    hsl = slice(half * 8, (half + 1) * 8)
                    bks = {}
                    for par in range(2):
                        bks[par] = (self.bank(), self.bank(), self.bank())
                    for n in (2 * blk, 2 * blk + 1):
                        po = (n % 2) * 64
                        for hh in range(8):
                            h = half * 8 + hh
                            hp, ho, par, a = h // 2, (h % 2) * 64, h % 2, hh // 2
                            b_ba, b_ka, b_a0 = bks[par]
                            arv = AR[ho:ho + 64, hp, n, :, :]
                            self.MM(ps[po:po + 64, b_ba, a * 128:(a + 1) * 128].rearrange("p (a x) -> p a x", x=64),
                                    Bf[ho:ho + 64, hp, n * 64:(n + 1) * 64], arv, True, True, [Bf.r, AR.r], [PB[b_ba]])
                            self.MM(ps[po:po + 64, b_ka, a * 128:(a + 1) * 128].rearrange("p (a x) -> p a x", x=64),
                                    Kf[ho:ho + 64, hp, n * 64:(n + 1) * 64], arv, True, True, [Kf.r, AR.r], [PB[b_ka]])
                            self.MM(ps[po:po + 64, b_a0, a * 64:(a + 1) * 64], AR[ho:ho + 64, hp, n, 0, :],
                                    Bf[ho:ho + 64, hp, n * 64:(n + 1) * 64], True, True, [AR.r, Bf.r], [PB[b_a0]])
                    for par in range(2):
                        b_ba, b_ka, b_a0 = bks[par]
                        m4b = mask2[:, 0:128].unsqueeze(1).to_broadcast([128, 4, 128])
                        m4a = mask2[:, 128:192].unsqueeze(1).to_broadcast([128, 4, 64])
                        hv0 = lambda t_: t_[:, 0, hsl, :].rearrange("p (a two) x -> p two a x", two=2)[:, par]
                        hv = lambda t_: t_[:, blk, hsl, :].rearrange("p (a two) x -> p two a x", two=2)[:, par]
                        self.TT("dve", hv(BAs), ps[:, b_ba, :].rearrange("p (a x) -> p a x", x=128), m4b, ALU.mult, [PB[b_ba], mask2.r], [BAs.rq[blk][half]])
                        self.TT("dve", hv(KAs), ps[:, b_ka, :].rearrange("p (a x) -> p a x", x=128), m4b, ALU.mult, [PB[b_ka], mask2.r], [KAs.rq[blk][half]])
                        self.TT("dve", hv0(NA[0]), ps[:, b_a0, 0:256].rearrange("p (a x) -> p a x", x=64), m4a, ALU.mult, [PB[b_a0], mask2.r], [NA[0].rq[0][half]])
                    self.TT("pool", NT[0][:, 0, hsl, :], BAs[:, blk, hsl, 0:64], idn64[:].unsqueeze(1).to_broadcast([128, 8, 64]), ALU.add,
                            [BAs.rq[blk][half], idn64.r], [NT[0].rq[0][half]])
                    yield
                for lvl in range(5):
                    Ak, An = NA[lvl % 2], NA[(lvl + 1) % 2]
                    Bn = NB[(lvl + 1) % 2]
                    Tk, Tn = NT[lvl % 2], NT[(lvl + 1) % 2]
                    for (blk, half) in units:
                        hsl = slice(half * 8, (half + 1) * 8)
                        bA, bB, bT = self.bank(), self.bank(), self.bank()
                        pA = ps[:, bA, :].rearrange("p (h x) -> p h x", x=64)
                        pB = ps[:, bB, :].rearrange("p (h x) -> p h x", x=64)
                        pT = ps[:, bT, :].rearrange("p (h x) -> p h x", x=64)

                        def Bk_ap(po, h):
                            if lvl == 0:
                                return BAs[po:po + 64, blk, h, 0:64]
                            return NB[lvl % 2][po:po + 64, 0, h, :]
                        Bk_r = BAs.rq[blk][half] if lvl == 0 else NB[lvl % 2].rq[0][half]
                        for hh in range(8):
                            h = half * 8 + hh
                            for po in (0, 64):
                                self.MM(pA[po:po + 64, hh, :], Bk_ap(po, h), Ak[po:po + 64, 0, h, :], True, True, [Bk_r, Ak.rq[0][half]], [PB[bA]])
                            if lvl < 4:
                                for po in (0, 64):
                                    self.MM(pB[po:po + 64, hh, :], Ak[po:po + 64, 0, h, :], Bk_ap(po, h), True, True, [Bk_r, Ak.rq[0][half]], [PB[bB]])
                        self.CPY("act", An[:, 0, hsl, :], pA, [PB[bA]], [An.rq[0][half]])
                        if lvl < 4:
                            self.CPY("act", Bn[:, 0, hsl, :], pB, [PB[bB]], [Bn.rq[0][half]])
                        for hh in range(8):
                            h = half * 8 + hh
                            for po in (0, 64):
                                self.MM(pT[po:po + 64, hh, :], An[po:po + 64, 0, h, :], Tk[po:po + 64, 0, h, :], True, True, [An.rq[0][half], Tk.rq[0][half]], [PB[bT]])
                        self.TT("dve", Tn[:, 0, hsl, :], pT, Tk[:, 0, hsl, :], ALU.add, [PB[bT], Tk.rq[0][half]], [Tn.rq[0][half]])
                        yield
                TTf = NT[1]
                for (blk, half) in units:
                    hsl = slice(half * 8, (half + 1) * 8)
                    bP = (self.bank(), self.bank())
                    bV = self.bank()
                    pV = ps[:, bV, :].rearrange("p (h x) -> p h x", x=64)
                    for hh in range(8):
                      for n in (2 * blk, 2 * blk + 1):
                        po = (n % 2) * 64
                        pP = ps[:, bP[n % 2], :].rearrange("p (q a x) -> p q a x", q=2, x=64)
                        if True:
                            h = half * 8 + hh
                            ho = (h % 2) * 64
                            fs = slice(h * 64, (h + 1) * 64)
                            self.MM(pP[ho:ho + 64, 0, hh // 2, :], Atm[po:po + 64, blk, fs], TTf[po:po + 64, 0, h, :], True, True,
                                    [Atm.r, TTf.rq[0][half]], [PB[bP[n % 2]]])
                            self.MM(pV[po:po + 64, hh, :], KAs[po:po + 64, blk, h, 0:64], Vtm[po:po + 64, blk, fs], True, True,
                                    [KAs.rq[blk][half], Vtm.r], [PB[bV]])
                            self.MM(pP[ho:ho + 64, 1, hh // 2, :], Ktm[po:po + 64, blk, fs], Vtm[po:po + 64, blk, fs], True, True,
                                    [Ktm.r, Vtm.r], [PB[bP[n % 2]]])
                    for n in (2 * blk, 2 * blk + 1):
                        pP = ps[:, bP[n % 2], :].rearrange("p (q a x) -> p q a x", q=2, x=64)
                        self.CPY("act", M1T[:, half * 4:(half + 1) * 4, n, :], pP[:, 0], [PB[bP[n % 2]]], [M1T.rq[blk][half]])
                        self.CPY("act", KVs[:, half * 4:(half + 1) * 4, n, :], pP[:, 1], [PB[bP[n % 2]]], [KVs.rq[blk][half]])
                    self.CPY("act", AKVs[:, blk, hsl, :], pV, [PB[bV]], [AKVs.rq[blk][half]])
                    bU = self.bank()
                    pU = ps[:, bU, :].rearrange("p (h x) -> p h x", x=64)
                    for hh in range(8):
                        for po in (0, 64):
                            h = half * 8 + hh
                            self.MM(pU[po:po + 64, hh, :], TTf[po:po + 64, 0, h, :], AKVs[po:po + 64, blk, h, :], True, True,
                                    [TTf.rq[0][half], AKVs.rq[blk][half]], [PB[bU]])
                    self.CPY("act", U2s[:, blk, hsl, :], pU, [PB[bU]], [U2s.rq[blk][half]])
                    yield
            for n in range(4):
                blk, po = n // 2, (n % 2) * 64
                Hbf, Hnx = Hbfs[(ti * 4 + n) % 2], Hbfs[(ti * 4 + n + 1) % 2]
                bU = bank2()
                pU4 = ps[:, bU:bU + 2, :].rearrange("p a (h x) -> p a h x", x=64)
                for h in range(16):
                    hp, ho = h // 2, (h % 2) * 64
                    self.MM(pU4[po:po + 64, h % 2, hp, :], M1T[ho:ho + 64, hp, n, :], Hbf[ho:ho + 64, hp, :], True, True,
                            [M1T.rq[blk][h // 8], Hbf.r], [PB[bU + (h % 2)]])
                hq = lambda t_: t_[po:po + 64, blk, :, :].rearrange("p (hp two) v -> p two hp v", two=2)
                self.TT("dve", hq(Utm), pU4[po:po + 64], hq(U2s), ALU.add, [PB[bU], PB[bU + 1], *U2s.rq[blk]], [*Utm.rq[blk]])
                bH = self.bank()
                pH = ps[:, bH, :].rearrange("p (h x) -> p h x", x=64)
                for h in range(16):
                    hp, ho = h // 2, (h % 2) * 64
                    self.MM(pH[ho:ho + 64, hp, :], Btm[po:po + 64, blk, h * 64:(h + 1) * 64], Utm[po:po + 64, blk, h, :], True, True,
                            [Btm.r, Utm.rq[blk][h // 8]], [PB[bH]])
                pcb = PC[:, :, n:n + 1].to_broadcast([128, 8, 64])
                self.TT("pool", HK[:], H32[:], KVs[:, :, n, :], ALU.add, [H32.r, *KVs.rq[blk]], [HK.r])
                self.TT("dve", T1[:], pH, HK[:], ALU.add, [PB[bH], HK.r], [T1.r])
                self.TT("dve", Hnx[:], T1[:], pcb, ALU.mult, [T1.r, PC.r], [Hnx.r])
                self.TT("pool", H32[:], T1[:], pcb, ALU.mult, [T1.r, PC.r], [H32.r])
                bY1, bY2 = self.bank(), self.bank()
                pY1 = ps[:, bY1, :].rearrange("p (h x) -> p h x", x=64)
                pY2 = ps[:, bY2, :].rearrange("p (h x) -> p h x", x=64)
                for h in range(16):
                    hp, ho = h // 2, (h % 2) * 64
                    self.MM(pY1[ho:ho + 64, hp, :], Hbf[ho:ho + 64, hp, :], AR[ho:ho + 64, hp, n, 1, :], True, True, [Hbf.r, AR.r], [PB[bY1]])
                for h in range(16):
                    hp, ho = h // 2, (h % 2) * 64
                    yo = pY2[ho:ho + 64, hp, :]
                    self.MM(yo, Utm[po:po + 64, blk, h, :], BAs[po:po + 64, blk, h, 64:128], True, False, [Utm.rq[blk][h // 8], BAs.rq[blk][h // 8]], [PB[bY2]])
                    self.MM(yo, Vtm[po:po + 64, blk, h * 64:(h + 1) * 64], KAs[po:po + 64, blk, h, 64:128], False, True, [Vtm.r, KAs.rq[blk][h // 8]], [PB[bY2]])
                self.CPY("act", HK[:], pY1, [PB[bY1]], [HK.r])
                self.TT("dve", Yf[:, :, n * 64:(n + 1) * 64], pY2, HK[:], ALU.add, [PB[bY2], HK.r], [Yf.r])
                yield
            for bq in range(4):
                yc, rst, yn, ysq = Hy, T1y, ynb, tsq[1]
                c0 = 2 * bq
                p3 = lambda b: ps[:, b, :].rearrange("p (q t) -> p q t", q=2)
                bm = self.bank()
                for q in range(2):
                    self.MM(ps[:, bm, q * TW:(q + 1) * TW], bones[:], Yf[:, c0 + q, :], True, True, [bones.r, Yf.r], [PB[bm]])
                self.STT("dve", yc[:], p3(bm), -1.0 / 64, Yf[:, c0:c0 + 2, :], ALU.mult, ALU.add, [PB[bm], Yf.r], [yc.r])
                self.ACT(ysq[:], yc[:], AF.Square, [yc.r], [ysq.r])
                bv_ = self.bank()
                for q in range(2):
                    self.MM(ps[:, bv_, q * TW:(q + 1) * TW], bones_b[:], ysq[:, q, :], True, True, [bones_b.r, ysq.r], [PB[bv_]])
                self.ACT(rst[:], p3(bv_), AF.Ln, [PB[bv_], self.epsc.r], [rst.r], bias=self.epsc[:, 1:2], scale=1.0 / 64)
                self.ACT(rst[:], rst[:], AF.Exp, [rst.r], [rst.r], scale=-0.5)
                self.TT("pool", yn[:], yc[:], rst[:], ALU.mult, [yc.r, rst.r], [yn.r])
                for q in range(2):
                    self.ACT(yn[:, q, :], yn[:, q, :], AF.Identity, [yn.r, self.colp.r], [yn.r], bias=self.col("lnb", c0 + q), scale=self.col("lnw", c0 + q))
                self.TT("dve", yn[:], yn[:], bonus[:, c0:c0 + 2, :], ALU.add, [yn.r, bonus.r], [yn.r])
                YO = YOb[bq % 2]
                self.TT("dve", YO[:], yn[:], Gt[:, c0:c0 + 2, :], ALU.mult, [yn.r, Gt.r], [YO.r])
                self.DMA("sp", self.YRT[c0 * 128:(c0 + 2) * 128, t0:t0 + TW].rearrange("(c p) t -> p c t", p=128), YO[:], [YO.r], [Region()])
                yield
        if os.environ.get('KD_SCHED', '1') == '1':
            P.defer = True
        for _ in P_gen(0):
            pass
        for ti in range(ntile):
            p = P_gen(ti + 1) if ti + 1 < ntile else None
            for _ in Q_gen(ti):
                if p is not None:
                    try:
                        next(p)
                    except StopIteration:
                        p = None
            if p is not None:
                for _ in p:
                    pass
        if P.defer:
            P.schedule()
        self.P.barrier()
        self.P.flush()


KB.phaseD = _phaseD


TE = 256


def _phaseE(self, hT2):
    ps, PB = self.ps, self.PB
    with ExitStack() as st:
        sb = lambda shape, dt, name: self.sb(st, shape, dt, name)
        wao = sb([128, 4, D], BF16, "wao")
        wro = sb([128, 8, D], BF16, "wro")
        wo = sb([128, 8, D], BF16, "wo")
        with ExitStack() as st2:
            wstg = [self.sb(st2, [128, 4, D], F32, "wstg") for _ in range(2)]
            i = 0
            for dst, src, nk in ((wao, self.w_att_out, 4), (wro, self.w_rwkv_out, 8), (wo, self.w_o, 8)):
                for k0 in range(0, nk, 4):
                    wsg = wstg[i % 2]
                    i += 1
                    self.DMA("sp", wsg[:], src[k0 * 128:(k0 + 4) * 128, :].rearrange("(k p) n -> p k n", p=128), (), [wsg.r])
                    self.CPY("pool", dst[:, k0:k0 + 4, :], wsg[:], [wsg.r], [dst.r])
            self.P.barrier()
            self.P.flush()
        if os.environ.get('KD_SCHED', '1') == '1':
            self.P.defer = True
        yat = [sb([128, 4, TE], BF16, "yat") for _ in range(2)]
        yrt = [sb([128, 8, TE], BF16, "yrt") for _ in range(2)]
        gat = [sb([128, 8, TE], BF16, "gat") for _ in range(2)]
        grt = [sb([128, 8, TE], BF16, "grt") for _ in range(2)]
        ga = [sb([128, TE], F32, "ga") for _ in range(2)]
        gr = [sb([128, TE], F32, "gr") for _ in range(2)]
        m1 = [sb([128, TE], F32, "m1") for _ in range(2)]
        m2 = [sb([128, TE], F32, "m2") for _ in range(2)]
        mix = [sb([128, 8, TE], BF16, "mix") for _ in range(2)]
        xt = [sb([128, D], F32, "xt") for _ in range(2)]
        x1t = [sb([128, D], F32, "x1t") for _ in range(2)]
        tz = [sb([128, 512], F32, "tz") for _ in range(2)]
        xs = [sb([128, D], F32, "xs") for _ in range(4)]
        junk = sb([128, D], F32, "junk")
        ssq = [sb([128, 1], F32, "ssq") for _ in range(2)]
        rs = [sb([128, 1], F32, "rs") for _ in range(2)]
        k = 0
        bi = 0
        for ti in range(S // TE):
            t0 = ti * TE
            ya_, yr_, ga_l, gr_l, mx = yat[ti % 2], yrt[ti % 2], gat[ti % 2], grt[ti % 2], mix[ti % 2]
            tsl = slice(t0, t0 + TE)
            self.DMA("sp", ya_[:], self.YAT[:, tsl].rearrange("(k p) t -> p k t", p=128), (), [ya_.r])
            self.DMA("sp", yr_[:], self.YRT[:, tsl].rearrange("(k p) t -> p k t", p=128), (), [yr_.r])
            self.DMA("pool", ga_l[:], self.PF[GATE0:GATE0 + D, tsl].rearrange("(k p) t -> p k t", p=128), (), [ga_l.r])
            self.DMA("pool", gr_l[:], self.PF[GATE0 + D:GATE0 + 2 * D, tsl].rearrange("(k p) t -> p k t", p=128), (), [gr_l.r])
            for oc in range(8):
                ocs = slice(oc * 128, (oc + 1) * 128)
                b = self.bank()
                for kc in range(4):
                    self.MM(ps[:, b, 0:TE], wao[:, kc, ocs], ya_[:, kc, :], kc == 0, kc == 3, [wao.r, ya_.r], [PB[b]])
                for kc in range(8):
                    self.MM(ps[:, b, TE:2 * TE], wro[:, kc, ocs], yr_[:, kc, :], kc == 0, kc == 7, [wro.r, yr_.r], [PB[b]])
                g1, g2_, a1, a2_ = ga[k % 2], gr[k % 2], m1[k % 2], m2[k % 2]
                k += 1
                self.ACT(g1[:], ga_l[:, oc, :], AF.Sigmoid, [ga_l.r, self.colp.r], [g1.r], bias=self.col("bgate", oc))
                self.ACT(g2_[:], gr_l[:, oc, :], AF.Sigmoid, [gr_l.r, self.colp.r], [g2_.r], bias=self.col("bgate", 8 + oc))
                self.TT("dve", a1[:], ps[:, b, 0:TE], g1[:], ALU.mult, [PB[b], g1.r], [a1.r])
                self.TT("dve", a2_[:], ps[:, b, TE:2 * TE], g2_[:], ALU.mult, [PB[b], g2_.r], [a2_.r])
                self.TT("pool", mx[:, oc, :], a1[:], a2_[:], ALU.add, [a1.r, a2_.r], [mx.r])
            pair = []
            for sbk in range(2):
                r0 = t0 + sbk * 128
                x_, x1_ = xt[bi % 2], x1t[bi % 2]
                self.DMA("sp", x_[:], self.x[r0:r0 + 128, :], (), [x_.r])
                for half in range(2):
                    hs_ = slice(half * 512, (half + 1) * 512)
                    b = self.bank()
                    for kc in range(8):
                        self.MM(ps[:, b, :], mx[:, kc, sbk * 128:(sbk + 1) * 128], wo[:, kc, hs_], kc == 0, kc == 7, [mx.r, wo.r], [PB[b]])
                    tz_ = tz[half]
                    self.TT("dve", tz_[:], ps[:, b, :], self.GT1[:, hs_], ALU.mult, [PB[b], self.GT1.r], [tz_.r])
                    self.TT("pool", x1_[:, hs_], tz_[:], x_[:, hs_], ALU.add, [tz_.r, x_.r], [x1_.r])
                self.DMA("sp", self.X1[r0:r0 + 128, :], x1_[:], [x1_.r], [Region()])
                xs_ = xs[bi % 4]
                self.norm_block(x1_, xs_, junk, ssq[bi % 2], rs[bi % 2])
                pair.append(xs_)
                bi += 1
            self.transpose_group(pair, ti, self.A2c, self.S2c, hT2)
        if self.P.defer:
            self.P.schedule()
        self.P.barrier()
        self.P.flush()


def _phaseF1(self, hT2):
    ps, PB = self.ps, self.PB
    NF = DFF // 128
    with ExitStack() as st:
        sb = lambda shape, dt, name: self.sb(st, shape, dt, name)
        wf = [sb([128, 8, 256], F32, "wf") for _ in range(2)]
        wb = [sb([128, 8, 256], BF16, "wb") for _ in range(2)]
        NBUF = 4
        UG = [sb([128, 514], F32, "UG") for _ in range(NBUF)]
        UV = [sb([128, 514], F32, "UV") for _ in range(NBUF)]
        cg = [sb([128, 512], F32, "cg") for _ in range(NBUF)]
        cv = [sb([128, 512], F32, "cv") for _ in range(NBUF)]
        sg = [sb([128, 512], F32, "sg") for _ in range(NBUF)]
        ao = [sb([128, 512], BF16, "ao") for _ in range(NBUF)]
        tpl = [sb([128, 512], F32, "tpl") for _ in range(NBUF)]
        tpl2 = [sb([128, 512], F32, "tpl2") for _ in range(NBUF)]
        if os.environ.get('KD_SCHED', '1') == '1':
            self.P.defer = True
        UGh = [Region("ugh") for _ in range(NBUF)]
        UVh = [Region("uvh") for _ in range(NBUF)]
        k = 0
        pending = None
        for f in range(NF):
            f_, b_ = wf[f % 2], wb[f % 2]
            src = self.w_up.rearrange("(k p) n -> p k n", p=128)
            self.DMA("sp", f_[:, :, 0:128], src[:, :, f * 128:(f + 1) * 128], (), [f_.r])
            self.DMA("sp", f_[:, :, 128:256], src[:, :, DFF + f * 128:DFF + (f + 1) * 128], (), [f_.r])
            self.CPY("dve", b_[:, 0:4, :], f_[:, 0:4, :], [f_.r], [b_.r])
            self.CPY("dve", b_[:, 4:8, :], f_[:, 4:8, :], [f_.r], [b_.r])
            for tt in range(8):
                t0 = tt * 512
                ug, uv = UG[k % NBUF], UV[k % NBUF]
                ugp, uvp = UG[(k - 1) % NBUF], UV[(k - 1) % NBUF]
                cg_, cv_, sg_, ao_ = cg[k % NBUF], cv[k % NBUF], sg[k % NBUF], ao[k % NBUF]
                k += 1
                bg, bv = self.bank(), self.bank()
                for kc in range(8):
                    self.MM(ps[:, bg, :], b_[:, kc, 0:128], hT2[:, kc, t0:t0 + 512], kc == 0, kc == 7, [b_.r, hT2.r], [PB[bg]])
                for kc in range(8):
                    self.MM(ps[:, bv, :], b_[:, kc, 128:256], hT2[:, kc, t0:t0 + 512], kc == 0, kc == 7, [b_.r, hT2.r], [PB[bv]])
                ugh, uvh = UGh[(k - 1) % NBUF], UVh[(k - 1) % NBUF]
                if tt == 0:
                    self.MSET("pool", ug[:, 0:2], 0.0, [ugh])
                    self.MSET("pool", uv[:, 0:2], 0.0, [uvh])
                else:
                    self.CPY("dve", ug[:, 0:2], ugp[:, 512:514], [ugp.r], [ugh])
                    self.CPY("dve", uv[:, 0:2], uvp[:, 512:514], [uvp.r], [uvh])
                self.CPY("act", ug[:, 2:514], ps[:, bg, :], [PB[bg]], [ug.r])
                self.CPY("act", uv[:, 2:514], ps[:, bv, :], [PB[bv]], [uv.r])
                self.TS("dve", cg_[:], ug[:, 2:514], self.col("cw2", f), self.col("cb", f), ALU.mult, ALU.add, [ug.r, self.colp.r], [cg_.r])
                self.STT("dve", cg_[:], ug[:, 1:513], self.col("cw1", f), cg_[:], ALU.mult, ALU.add, [ug.r, ugh, cg_.r, self.colp.r], [cg_.r])
                self.STT("dve", cg_[:], ug[:, 0:512], self.col("cw0", f), cg_[:], ALU.mult, ALU.add, [ug.r, ugh, cg_.r, self.colp.r], [cg_.r])
                jf = NF + f
                tp_ = tpl[k % NBUF]
                tq_ = tpl2[k % NBUF]
                self.ACT(cv_[:], uv[:, 2:514], AF.Identity, [uv.r, self.colp.r], [cv_.r], bias=self.col("cb", jf), scale=self.col("cw2", jf))
                self.ACT(tp_[:], uv[:, 1:513], AF.Identity, [uv.r, uvh, self.colp.r], [tp_.r], scale=self.col("cw1", jf))
                self.ACT(tq_[:], uv[:, 0:512], AF.Identity, [uv.r, uvh, self.colp.r], [tq_.r], scale=self.col("cw0", jf))
                self.TT("pool", tp_[:], tp_[:], tq_[:], ALU.add, [tp_.r, tq_.r], [tp_.r])
                self.TT("pool", cv_[:], cv_[:], tp_[:], ALU.add, [cv_.r, tp_.r], [cv_.r])
                if pending is not None:
                    pending()

                def tail(cg_=cg_, cv_=cv_, sg_=sg_, ao_=ao_, f=f, t0=t0):
                    self.ACT(sg_[:], cg_[:], AF.Silu, [cg_.r], [sg_.r])
                    self.TT("dve", ao_[:], sg_[:], cv_[:], ALU.mult, [sg_.r, cv_.r], [ao_.r])
                    self.DMA("sp", self.ACTS[f * 128:(f + 1) * 128, t0:t0 + 512], ao_[:], [ao_.r], [Region()])
                pending = tail
        pending()
        if self.P.defer:
            self.P.schedule()
        self.P.barrier()
        self.P.flush()


def _phaseF2(self):
    ps, PB = self.ps, self.PB
    NF = DFF // 128
    with ExitStack() as st:
        sb = lambda shape, dt, name: self.sb(st, shape, dt, name)
        wd = sb([128, NF, D], BF16, "wd")
        wstg = [sb([128, 2, D], F32, "wstg") for _ in range(2)]
        for i, k0 in enumerate(range(0, NF, 2)):
            wsg = wstg[i % 2]
            self.DMA("sp", wsg[:], self.w_down[k0 * 128:(k0 + 2) * 128, :].rearrange("(k p) n -> p k n", p=128), (), [wsg.r])
            self.CPY("pool", wd[:, k0:k0 + 2, :], wsg[:], [wsg.r], [wd.r])
        if os.environ.get('KD_SCHED', '1') == '1':
            self.P.defer = True
        NFW = sb([128, D], F32, "nfw")
        self.DMA("sp", NFW[:], self.norm_f_w.partition_broadcast(128), (), [NFW.r])
        at = [sb([128, NF, 512], BF16, "at") for _ in range(2)]
        x1t = [sb([128, D], F32, "x1t") for _ in range(2)]
        x2t = [sb([128, D], F32, "x2t") for _ in range(2)]
        xs = [sb([128, D], F32, "xs") for _ in range(2)]
        ot = [sb([128, D], F32, "ot") for _ in range(2)]
        tz = [sb([128, 512], F32, "tz") for _ in range(2)]
        junk = sb([128, D], F32, "junk")
        ssq = [sb([128, 1], F32, "ssq") for _ in range(2)]
        rs = [sb([128, 1], F32, "rs") for _ in range(2)]
        outs = []
        bi = 0
        for tt in range(8):
            t0 = tt * 512
            a_ = at[tt % 2]
            self.DMA("pool", a_[:], self.ACTS[:, t0:t0 + 512].rearrange("(k p) t -> p k t", p=128), (), [a_.r])
            for sbk in range(4):
                r0 = t0 + sbk * 128
                x1_, x2_, xs_, o_ = x1t[bi % 2], x2t[bi % 2], xs[bi % 2], ot[bi % 2]
                self.DMA("sp", x1_[:], self.X1[r0:r0 + 128, :], (), [x1_.r])
                for half in range(2):
                    hs_ = slice(half * 512, (half + 1) * 512)
                    b = self.bank()
                    for kc in range(NF):
                        self.MM(ps[:, b, :], a_[:, kc, sbk * 128:(sbk + 1) * 128], wd[:, kc, hs_], kc == 0, kc == NF - 1, [a_.r, wd.r], [PB[b]])
                    tz_ = tz[half]
                    self.TT("dve", tz_[:], ps[:, b, :], self.GT2[:, hs_], ALU.mult, [PB[b], self.GT2.r], [tz_.r])
                    self.TT("pool", x2_[:, hs_], tz_[:], x1_[:, hs_], ALU.add, [tz_.r, x1_.r], [x2_.r])
                self.norm_block(x2_, xs_, junk, ssq[bi % 2], rs[bi % 2])
                self.TT("dve", o_[:], xs_[:], NFW[:], ALU.mult, [xs_.r, NFW.r], [o_.r])
                rg = Region()
                outs.append(rg)
                self.DMA("sp", self.out[r0:r0 + 128, :], o_[:], [o_.r], [rg])
                bi += 1
        if self.P.defer:
            self.P.schedule()
        self.P.finish(outs)
        self.P.barrier()
        self.P.flush()


KB.phaseE = _phaseE
KB.phaseF1 = _phaseF1
KB.phaseF2 = _phaseF2
```
